# Optimizing a Trainium2 kernel written in Bass

```python
import math
import jax, jax.numpy as jnp
from jax import lax
import numpy as np

D_MODEL = 2048
BATCH = 32
SEQ = 256
DEPTH = 4
DEC_BATCH = 8
DEC_SEQ = 1024
PAST_LEN = 512

GRID_W = 64
N_HEADS = 16
NOPE_DIM = 128
ROPE_DIM = 64
V_DIM = 128
QK_DIM = NOPE_DIM + ROPE_DIM
Q_LORA = 512
KV_LORA = 512
ROPE_PAIRS = ROPE_DIM // 4
ROPE_BASE = 10000.0
Q_BLOCK = 128
CONV_W = 1024
CONV_K = 31
CONV_PAD = (CONV_K - 1) // 2
SSM_W = 1024
SSM_GROUP_CH = 16
SSM_GROUPS = SSM_W // SSM_GROUP_CH
SSM_STATE = 64
D_FF = 4 * D_MODEL
N_BRANCH = 3
IN_COLS = Q_LORA + KV_LORA + ROPE_DIM + 2 * CONV_W + SSM_W + N_BRANCH * D_MODEL
EPS = 1e-6

kernel_name = 'hybrid_mla_conv_s5_diffusion_step'


def _rms(x, g):
    xf = x.astype(jnp.float32)
    y = xf * lax.rsqrt(jnp.mean(xf * xf, axis=-1, keepdims=True) + EPS)
    return (y * g.astype(jnp.float32)).astype(x.dtype)


def _axial_rope_tables(n_tok):
    rows = n_tok // GRID_W
    row = jnp.repeat(jnp.arange(rows), GRID_W).astype(jnp.float32)
    col = jnp.tile(jnp.arange(GRID_W), rows).astype(jnp.float32)
    inv = ROPE_BASE ** (-jnp.arange(ROPE_PAIRS, dtype=jnp.float32) / ROPE_PAIRS)
    ang = jnp.stack([row[:, None] * inv, col[:, None] * inv], axis=1)
    return jnp.cos(ang), jnp.sin(ang)


def _rope(x, cos, sin):
    extra = x.ndim - 3
    shp = (cos.shape[0],) + (1,) * extra + (2, ROPE_PAIRS)
    cs, sn = cos.reshape(shp), sin.reshape(shp)
    xf = x.astype(jnp.float32).reshape(x.shape[:-1] + (2, 2, ROPE_PAIRS))
    x1, x2 = xf[..., 0, :], xf[..., 1, :]
    out = jnp.stack([x1 * cs - x2 * sn, x2 * cs + x1 * sn], axis=-2)
    return out.reshape(x.shape).astype(x.dtype)


def _attend(q, k, v):
    b, sq, h, dq = q.shape
    nb = sq // Q_BLOCK
    qb = q.reshape(b, nb, Q_BLOCK, h, dq).swapaxes(0, 1)
    scale = 1.0 / math.sqrt(QK_DIM)

    def one(qi):
        s = jnp.einsum('bqhd,bkhd->bhqk', qi, k).astype(jnp.float32) * scale
        pr = jax.nn.softmax(s, axis=-1).astype(v.dtype)
        return jnp.einsum('bhqk,bkhd->bqhd', pr, v)

    o = lax.map(one, qb)
    return o.swapaxes(0, 1).reshape(b, sq, h, v.shape[-1])


def _conv_branch(z2, p):
    a, b = jnp.split(z2, 2, axis=-1)
    z = a * jax.nn.sigmoid(b)
    z = lax.conv_general_dilated(z, p['conv_w'][:, None, :], window_strides=(1,),
                                 padding=[(CONV_PAD, CONV_PAD)],
                                 dimension_numbers=('NWC', 'WIO', 'NWC'),
                                 feature_group_count=CONV_W) + p['conv_b']
    zf = z.astype(jnp.float32)
    mu = jnp.mean(zf, axis=-1, keepdims=True)
    var = jnp.mean(jnp.square(zf - mu), axis=-1, keepdims=True)
    z = ((zf - mu) * lax.rsqrt(var + EPS) * p['conv_ln_g'].astype(jnp.float32)
         + p['conv_ln_b'].astype(jnp.float32)).astype(z2.dtype)
    return jax.nn.silu(z) @ p['w_conv_o']


def _cmul_combine(e1, e2):
    a1r, a1i, b1r, b1i = e1
    a2r, a2i, b2r, b2i = e2
    return (a1r * a2r - a1i * a2i, a1r * a2i + a1i * a2r,
            a2r * b1r - a2i * b1i + b2r, a2r * b1i + a2i * b1r + b2i)


def _ssm_branch(u, p, h0_re, h0_im):
    bsz, L, _ = u.shape
    uf = u.astype(jnp.float32).reshape(bsz, L, SSM_GROUPS, SSM_GROUP_CH)
    y = p['ssm_d'].astype(jnp.float32).reshape(SSM_GROUPS, SSM_GROUP_CH) * uf
    fins_re, fins_im = [], []
    for d in range(2):
        a_re = p['ssm_a_re'][d].astype(jnp.float32)
        a_im = p['ssm_a_im'][d].astype(jnp.float32)
        dt = jnp.exp(p['ssm_log_dt'][d].astype(jnp.float32))[:, None]
        mag = jnp.exp(a_re * dt)
        lb_re, lb_im = mag * jnp.cos(a_im * dt), mag * jnp.sin(a_im * dt)
        den = a_re * a_re + a_im * a_im
        nr = lb_re - 1.0
        f_re = (nr * a_re + lb_im * a_im) / den
        f_im = (lb_im * a_re - nr * a_im) / den
        b_re = p['ssm_b_re'][d].astype(jnp.float32)
        b_im = p['ssm_b_im'][d].astype(jnp.float32)
        bb_re = f_re[..., None] * b_re - f_im[..., None] * b_im
        bb_im = f_re[..., None] * b_im + f_im[..., None] * b_re
        ud = uf if d == 0 else jnp.flip(uf, axis=1)
        bu_re = jnp.einsum('blgc,gpc->blgp', ud, bb_re)
        bu_im = jnp.einsum('blgc,gpc->blgp', ud, bb_im)
        if h0_re is not None:
            hr = h0_re[:, d].astype(jnp.float32)
            hi = h0_im[:, d].astype(jnp.float32)
            bu_re = bu_re.at[:, 0].add(lb_re * hr - lb_im * hi)
            bu_im = bu_im.at[:, 0].add(lb_re * hi + lb_im * hr)
        ar = jnp.broadcast_to(lb_re, bu_re.shape)
        ai = jnp.broadcast_to(lb_im, bu_im.shape)
        _, _, xr, xi = lax.associative_scan(_cmul_combine, (ar, ai, bu_re, bu_im), axis=1)
        fins_re.append(xr[:, -1])
        fins_im.append(xi[:, -1])
        c_re = p['ssm_c_re'][d].astype(jnp.float32)
        c_im = p['ssm_c_im'][d].astype(jnp.float32)
        yd = jnp.einsum('blgp,gcp->blgc', xr, c_re) - jnp.einsum('blgp,gcp->blgc', xi, c_im)
        y = y + (yd if d == 0 else jnp.flip(yd, axis=1))
    y = y.reshape(bsz, L, SSM_W).astype(u.dtype)
    g = jax.nn.gelu(y)
    out = g * jax.nn.sigmoid(g @ p['w_glu'] + p['b_glu'])
    return out @ p['w_ssm_o'], jnp.stack(fins_re, axis=1), jnp.stack(fins_im, axis=1)


def _block(x, mod, p, lat):
    shift1, scale1, gate1, shift2, scale2, gate2 = jnp.split(mod, 6, axis=-1)
    bsz, L, _ = x.shape
    h = _rms(x, p['g_mix_pre']) * (1 + scale1) + shift1
    proj = h @ p['w_in']
    i1 = Q_LORA
    i2 = i1 + KV_LORA
    i3 = i2 + ROPE_DIM
    i4 = i3 + 2 * CONV_W
    i5 = i4 + SSM_W
    c_q, c_kv, k_pe, conv_in, ssm_in, gate_in = jnp.split(proj, [i1, i2, i3, i4, i5], axis=-1)
    q = (_rms(c_q, p['g_q']) @ p['w_uq']).reshape(bsz, L, N_HEADS, QK_DIM)
    ckv = _rms(c_kv, p['g_kv'])
    if lat is None:
        ckv_keys, kpe_keys = ckv, k_pe
        h0_re = h0_im = None
    else:
        cos, sin, ckv_ctx, kpe_ctx, h0_re, h0_im = lat
        q = jnp.concatenate([q[..., :NOPE_DIM], _rope(q[..., NOPE_DIM:], cos, sin)], axis=-1)
        ckv_keys = jnp.concatenate([ckv, ckv_ctx.astype(ckv.dtype)], axis=1)
        kpe_keys = jnp.concatenate([_rope(k_pe, cos, sin), kpe_ctx.astype(k_pe.dtype)], axis=1)
    kv = (ckv_keys @ p['w_ukv']).reshape(bsz, -1, N_HEADS, NOPE_DIM + V_DIM)
    sk = kv.shape[1]
    k = jnp.concatenate([kv[..., :NOPE_DIM],
                         jnp.broadcast_to(kpe_keys[:, :, None, :], (bsz, sk, N_HEADS, ROPE_DIM))], axis=-1)
    v = kv[..., NOPE_DIM:]
    attn = _attend(q, k, v).reshape(bsz, L, N_HEADS * V_DIM) @ p['w_attn_o']
    conv = _conv_branch(conv_in, p)
    ssm, fin_re, fin_im = _ssm_branch(ssm_in, p, h0_re, h0_im)
    g_a, g_c, g_s = jnp.split(jax.nn.sigmoid(gate_in), N_BRANCH, axis=-1)
    mix = (g_a * attn + g_c * conv + g_s * ssm) @ p['w_out']
    x = x + gate1 * _rms(mix, p['g_mix_post'])
    h2 = _rms(x, p['g_mlp_pre']) * (1 + scale2) + shift2
    f = jnp.square(jax.nn.relu(h2 @ p['w_ff1'])) @ p['w_ff2']
    x = x + gate2 * _rms(f, p['g_mlp_post'])
    return x, ckv, k_pe, fin_re, fin_im


def setup_inputs(seed: int = 0) -> dict:
    key = jax.random.key(seed)
    ks = iter(jax.random.split(key, 64))

    def nrm(shape, s):
        return jax.random.normal(next(ks), shape, jnp.float32) * s

    def gain(shape):
        return 1.0 + nrm(shape, 0.01)

    L = DEPTH
    n_idx = jnp.arange(SSM_STATE, dtype=jnp.float32)
    return {
        'x_prompt': nrm((BATCH, SEQ, D_MODEL), 1.0),
        'x_sample': nrm((DEC_BATCH, DEC_SEQ, D_MODEL), 1.0),
        'cache_ckv': nrm((DEC_BATCH, DEPTH, PAST_LEN, KV_LORA), 1.0),
        'cache_kpe': nrm((DEC_BATCH, DEPTH, PAST_LEN, ROPE_DIM), 1.0),
        'state_ssm_re': nrm((DEC_BATCH, DEPTH, 2, SSM_GROUPS, SSM_STATE), 0.5),
        'state_ssm_im': nrm((DEC_BATCH, DEPTH, 2, SSM_GROUPS, SSM_STATE), 0.5),
        'c': nrm((DEC_BATCH, D_MODEL), 1.0),
        'c_ctx': nrm((D_MODEL,), 1.0),
        'w_mod': nrm((L, D_MODEL, 6 * D_MODEL), 0.5 * D_MODEL ** -0.5),
        'b_mod': nrm((L, 6 * D_MODEL), 0.01),
        'g_mix_pre': gain((L, D_MODEL)),
        'g_mix_post': gain((L, D_MODEL)),
        'g_mlp_pre': gain((L, D_MODEL)),
        'g_mlp_post': gain((L, D_MODEL)),
        'w_in': nrm((L, D_MODEL, IN_COLS), D_MODEL ** -0.5),
        'g_q': gain((L, Q_LORA)),
        'w_uq': nrm((L, Q_LORA, N_HEADS * QK_DIM), Q_LORA ** -0.5),
        'g_kv': gain((L, KV_LORA)),
        'w_ukv': nrm((L, KV_LORA, N_HEADS * (NOPE_DIM + V_DIM)), KV_LORA ** -0.5),
        'w_attn_o': nrm((L, N_HEADS * V_DIM, D_MODEL), (N_HEADS * V_DIM) ** -0.5),
        'conv_w': nrm((L, CONV_K, CONV_W), CONV_K ** -0.5),
        'conv_b': nrm((L, CONV_W), 0.01),
        'conv_ln_g': gain((L, CONV_W)),
        'conv_ln_b': nrm((L, CONV_W), 0.01),
        'w_conv_o': nrm((L, CONV_W, D_MODEL), CONV_W ** -0.5),
        'ssm_a_re': -0.5 + nrm((L, 2, SSM_GROUPS, SSM_STATE), 0.01),
        'ssm_a_im': math.pi * n_idx + nrm((L, 2, SSM_GROUPS, SSM_STATE), 0.01),
        'ssm_log_dt': jax.random.uniform(next(ks), (L, 2, SSM_GROUPS), jnp.float32,
                                         minval=math.log(1e-3), maxval=math.log(1e-1)),
        'ssm_b_re': nrm((L, 2, SSM_GROUPS, SSM_STATE, SSM_GROUP_CH), (2 * SSM_GROUP_CH) ** -0.5),
        'ssm_b_im': nrm((L, 2, SSM_GROUPS, SSM_STATE, SSM_GROUP_CH), (2 * SSM_GROUP_CH) ** -0.5),
        'ssm_c_re': nrm((L, 2, SSM_GROUPS, SSM_GROUP_CH, SSM_STATE), (2 * SSM_STATE) ** -0.5),
        'ssm_c_im': nrm((L, 2, SSM_GROUPS, SSM_GROUP_CH, SSM_STATE), (2 * SSM_STATE) ** -0.5),
        'ssm_d': nrm((L, SSM_W), 1.0),
        'w_glu': nrm((L, SSM_W, SSM_W), SSM_W ** -0.5),
        'b_glu': nrm((L, SSM_W), 0.01),
        'w_ssm_o': nrm((L, SSM_W, D_MODEL), SSM_W ** -0.5),
        'w_out': nrm((L, D_MODEL, D_MODEL), D_MODEL ** -0.5),
        'w_ff1': nrm((L, D_MODEL, D_FF), D_MODEL ** -0.5),
        'w_ff2': nrm((L, D_FF, D_MODEL), D_FF ** -0.5),
    }


def reference(x_prompt, x_sample, cache_ckv, cache_kpe, state_ssm_re, state_ssm_im, c, c_ctx,
              w_mod, b_mod, g_mix_pre, g_mix_post, g_mlp_pre, g_mlp_post, w_in, g_q, w_uq,
              g_kv, w_ukv, w_attn_o, conv_w, conv_b, conv_ln_g, conv_ln_b, w_conv_o,
              ssm_a_re, ssm_a_im, ssm_log_dt, ssm_b_re, ssm_b_im, ssm_c_re, ssm_c_im, ssm_d,
              w_glu, b_glu, w_ssm_o, w_out, w_ff1, w_ff2):
    cos, sin = _axial_rope_tables(x_sample.shape[1])
    xp, xs = x_prompt, x_sample
    ckvs, kpes, srs, sis = [], [], [], []
    for l in range(DEPTH):
        p = {
            'g_mix_pre': g_mix_pre[l], 'g_mix_post': g_mix_post[l],
            'g_mlp_pre': g_mlp_pre[l], 'g_mlp_post': g_mlp_post[l],
            'w_in': w_in[l], 'g_q': g_q[l], 'w_uq': w_uq[l], 'g_kv': g_kv[l],
            'w_ukv': w_ukv[l], 'w_attn_o': w_attn_o[l],
            'conv_w': conv_w[l], 'conv_b': conv_b[l], 'conv_ln_g': conv_ln_g[l],
            'conv_ln_b': conv_ln_b[l], 'w_conv_o': w_conv_o[l],
            'ssm_a_re': ssm_a_re[l], 'ssm_a_im': ssm_a_im[l], 'ssm_log_dt': ssm_log_dt[l],
            'ssm_b_re': ssm_b_re[l], 'ssm_b_im': ssm_b_im[l],
            'ssm_c_re': ssm_c_re[l], 'ssm_c_im': ssm_c_im[l], 'ssm_d': ssm_d[l],
            'w_glu': w_glu[l], 'b_glu': b_glu[l], 'w_ssm_o': w_ssm_o[l],
            'w_out': w_out[l], 'w_ff1': w_ff1[l], 'w_ff2': w_ff2[l],
        }
        mod_ctx = (jax.nn.silu(c_ctx) @ w_mod[l] + b_mod[l])[None, None, :]
        mod_lat = (jax.nn.silu(c) @ w_mod[l] + b_mod[l])[:, None, :]
        xp, ckv, kpe, fr, fi = _block(xp, mod_ctx, p, None)
        ckvs.append(ckv)
        kpes.append(kpe)
        srs.append(fr)
        sis.append(fi)
        lat = (cos, sin, cache_ckv[:, l], cache_kpe[:, l], state_ssm_re[:, l], state_ssm_im[:, l])
        xs = _block(xs, mod_lat, p, lat)[0]
    new_ckv = jnp.stack(ckvs, axis=1)
    new_kpe = jnp.stack(kpes, axis=1)
    new_ssm_re = jnp.stack(srs, axis=1)
    new_ssm_im = jnp.stack(sis, axis=1)
    return (xp, xs, new_ckv, new_kpe, new_ssm_re, new_ssm_im)
```

```python
import math
import numpy as np
import concourse.bass as bass
import concourse.mybir as mybir
from concourse.bass_utils import run_bass_kernel_spmd

F32 = mybir.dt.float32
BF16 = mybir.dt.bfloat16
AF = mybir.ActivationFunctionType
ALU = mybir.AluOpType

D = 2048
L4 = 4
NT = 1024
NH = 16
EPS = 1e-6
IN_COLS = 10304
C_CQ, C_CKV, C_KPE, C_CVA, C_CVB, C_SSM, C_GA, C_GC, C_GS = 0, 512, 1024, 1088, 2112, 3136, 4160, 6208, 8256
PI = math.pi

V_GPRE, V_GPOST, V_GMPRE, V_GMPOST, V_BMOD, V_GQ, V_GKV, V_CB, V_LNG, V_LNB, V_SD, V_BGLU, V_CW = (
    0, 16, 32, 48, 64, 160, 164, 168, 176, 184, 192, 200, 208)
NV = 208 + 8 * 31


class Rec:
    def __init__(self):
        self.ops = []

    def op(self, eng, fn, r=(), w=(), chan=None):
        self.ops.append((eng, fn, tuple(r), tuple(w), chan))

    def barrier(self):
        self.ops.append((None, None, (), (), None))

    def emit(self, nc):
        ops = self.ops
        n = len(ops)
        engs = {'pe': nc.tensor, 'dve': nc.vector, 'act': nc.scalar, 'pool': nc.gpsimd, 'sp': nc.sync}
        last_w, readers = {}, {}
        deps = [None] * n
        last_on, pending = {}, {}
        for i, (eng, fn, r, w, chan) in enumerate(ops):
            if eng is None:
                snapb = set(last_on.values())
                for e_ in engs:
                    pending[e_] = pending.get(e_, set()) | snapb
                deps[i] = set()
                continue
            d = set()
            if eng in pending:
                d |= pending.pop(eng)
            for k in r:
                j = last_w.get(k)
                if j is not None:
                    d.add(j)
            for k in w:
                j = last_w.get(k)
                if j is not None:
                    d.add(j)
                rl = readers.get(k)
                if rl:
                    d.update(rl)
            d.discard(i)
            deps[i] = d
            for k in r:
                readers.setdefault(k, []).append(i)
            for k in w:
                last_w[k] = i
                readers[k] = []
            last_on[('c', chan) if chan is not None else eng] = i
        prod = [(('c', o[4]) if o[4] is not None else o[0]) for o in ops]
        clock = {e: {} for e in engs}
        snap = [None] * n
        waits = [None] * n
        signal = [False] * n
        for i, (eng, fn, r, w, chan) in enumerate(ops):
            if eng is None:
                waits[i] = []
                snap[i] = {}
                continue
            ck = clock[eng]
            need = []
            for j in sorted(deps[i], reverse=True):
                p = prod[j]
                if p == 'pe' and eng == 'pe' and chan is None:
                    continue
                if ck.get(p, -1) >= j:
                    continue
                need.append(j)
                ck[p] = j
                sj = snap[j]
                for kk, vv in sj.items():
                    if ck.get(kk, -1) < vv:
                        ck[kk] = vv
            for j in need:
                signal[j] = True
            waits[i] = need
            snap[i] = dict(ck)
        cnt = {}
        val = [0] * n
        for i, (eng, fn, r, w, chan) in enumerate(ops):
            if eng is None:
                continue
            if chan is not None:
                cnt[('c', chan)] = cnt.get(('c', chan), 0) + 16
                val[i] = cnt[('c', chan)]
            elif signal[i]:
                cnt[eng] = cnt.get(eng, 0) + 1
                val[i] = cnt[eng]
        sems = {}

        def sem_of(p):
            if p not in sems:
                nm = p if isinstance(p, str) else "c_" + str(p[1])
                sems[p] = nc.alloc_semaphore("s_" + nm)
            return sems[p]

        for i, (eng, fn, r, w, chan) in enumerate(ops):
            if eng is None:
                continue
            e = engs[eng]
            for j in waits[i]:
                e.wait_ge(sem_of(prod[j]), val[j])
            ins = fn(e)
            if chan is not None:
                ins.then_inc(sem_of(('c', chan)), 16)
            elif signal[i]:
                ins.then_inc(sem_of(eng), 1)
        for p, c in cnt.items():
            nc.sync.wait_ge(sem_of(p), c)
        return len(sems)


class Arena:
    _cnt = [0]

    def __init__(self, nc, base, size):
        self.nc, self.base, self.size, self.off, self.n = nc, base, size, 0, 0

    def reset(self, off=0):
        self.off = off

    def alloc(self, shape, dt):
        nb = int(np.prod(shape[1:])) * (2 if dt == BF16 else 4)
        nb = (nb + 31) // 32 * 32
        assert self.off + nb <= self.size, f"arena overflow {self.off}+{nb}>{self.size} for {shape}"
        Arena._cnt[0] += 1
        t = self.nc.alloc_sbuf_tensor_at(f"ar{Arena._cnt[0]}", list(shape), dt, offset=self.base + self.off)
        self.off += nb
        return t


def build_program(n_layers=L4, groups=(0, 1), debug=False):
    nc = bass.Bass("TRN2", target_bir_lowering=False)
    R = Rec()

    def din(name, shape):
        return nc.dram_tensor(name, list(shape), F32, kind="ExternalInput").ap()

    def dout(name, shape):
        return nc.dram_tensor(name, list(shape), F32, kind="ExternalOutput").ap()

    xT = din("xT", [2, D, NT])
    cT = din("cT", [128, 16, 2])
    ckvc = din("ckvc", [L4, 512, 512])
    kpec = din("kpec", [L4, 64, 512])
    h0d = din("h0", [L4, 2, 128, 64])
    w_mod = din("w_mod", [L4, D, 6 * D])
    w_in = din("w_in", [L4, D, IN_COLS])
    w_ksw = din("w_ksw", [L4, D, 64])
    w_head = din("w_head", [L4, NH, 512, 512])
    w_ao = din("w_ao", [L4, D, D])
    w_co = din("w_co", [L4, 1024, D])
    w_glu = din("w_glu", [L4, 1024, 1024])
    w_so = din("w_so", [L4, 1024, D])
    w_out = din("w_out", [L4, D, D])
    w_ff1 = din("w_ff1", [L4, D, 4 * D])
    w_ff2 = din("w_ff2", [L4, 4 * D, D])
    vecs_d = din("vecs", [128, L4, NV])
    ssm_s = din("ssm_s", [L4, 3, 128, 64])
    b_x = din("b_x", [L4, 2, 8, 2, 128, 512])
    c_x = din("c_x", [L4, 2, 32, 2, 128, 128])
    rope_d = din("rope", [2, 64, NT])
    ident_d = din("ident", [128, 128])
    yT = dout("yT", [2, D, NT])
    ckvT = dout("ckvT", [L4, 512, NT])
    kpeT = dout("kpeT", [L4, 64, NT])
    fin_d = dout("fin", [L4, 2, 128, 256])
    xs_d = nc.dram_tensor("xs_scratch", [D, NT], F32, kind="Internal").ap()

    base = (nc.sbuf_base + 31) // 32 * 32
    top = nc.sbuf_top
    off = [base]

    def salloc(name, shape, dt):
        nb = int(np.prod(shape[1:])) * (2 if dt == BF16 else 4)
        nb = (nb + 31) // 32 * 32
        t = nc.alloc_sbuf_tensor_at(name, list(shape), dt, offset=off[0])
        off[0] += nb
        return t

    H = salloc("H", [128, 16, NT], BF16)
    NSLOT = 2
    WS = [salloc(f"WS{i}", [128, 4096], BF16) for i in range(NSLOT)]
    gs_off = off[0]
    GS = [salloc(f"GS{i}", [128, 2048], BF16) for i in range(2)]
    GSW = nc.alloc_sbuf_tensor_at("GSW", [128, 4096], BF16, offset=gs_off)
    vecs = salloc("vecs", [128, L4, NV], F32)
    mods = salloc("mods", [128, L4, 96, 2], F32)
    mA = salloc("mA", [128, 2, 6, 16], F32)
    onesb = salloc("onesb", [128, 128], BF16)
    identb = salloc("identb", [128, 128], BF16)
    identf = salloc("identf", [128, 128], F32)
    onesf = salloc("onesf", [128, 128], F32)
    epsb = salloc("epsb", [128, 1], F32)
    negpi = salloc("negpi", [128, 1], F32)
    FIN = salloc("FIN", [128, 2, 4, 2, 32], F32)
    s2 = salloc("s2", [128, 16, 2], BF16)
    cTt = salloc("cTt", [128, 16, 2], F32)
    dbgs = salloc("dbgs", [128, 8], F32)
    xb_off = off[0]
    XB = salloc("XB", [128, 16, NT], F32)
    MIX = nc.alloc_sbuf_tensor_at("MIX", [128, 16, NT], BF16, offset=xb_off)
    ARX = Arena(nc, xb_off + 32768, 32768)
    AR = Arena(nc, off[0], top - off[0])
    PSt = nc.alloc_psum_tensor("PS", [128, 8, 512], F32)

    def PS(b):
        return PSt[:, b, :]

    def PK(b):
        return ('PS', b)

    def mm(out, lhsT, rhs, start, stop, r, w):
        R.op('pe', lambda e: e.matmul(out, lhsT=lhsT, rhs=rhs, start=start, stop=stop), r, w)

    def act(out, in_, func, r, w, bias=None, scale=None):
        kw = {}
        if bias is not None:
            kw['bias'] = bias
        if scale is not None:
            kw['scale'] = scale
        R.op('act', lambda e: e.activation(out=out, in_=in_, func=func, **kw), r, w)

    def tt(out, a, b, op, r, w, eng='dve'):
        R.op(eng, lambda e: e.tensor_tensor(out=out, in0=a, in1=b, op=op), r, w)

    def ts(out, a, s1, s2_, op0, op1, r, w, eng='dve'):
        if s2_ is None:
            R.op(eng, lambda e: e.tensor_scalar(out=out, in0=a, scalar1=s1, scalar2=None, op0=op0), r, w)
        else:
            R.op(eng, lambda e: e.tensor_scalar(out=out, in0=a, scalar1=s1, scalar2=s2_, op0=op0, op1=op1), r, w)

    def stt(out, a, s, b, op0, op1, r, w, eng='dve'):
        R.op(eng, lambda e: e.scalar_tensor_tensor(out=out, in0=a, scalar=s, in1=b, op0=op0, op1=op1), r, w)

    def cp(out, in_, r, w, eng='dve'):
        if eng == 'act':
            R.op(eng, lambda e: e.activation(out=out, in_=in_, func=AF.Copy), r, w)
        else:
            R.op(eng, lambda e: e.tensor_copy(out=out, in_=in_), r, w)

    def recip(out, in_, r, w):
        R.op('dve', lambda e: e.reciprocal(out=out, in_=in_), r, w)

    def mset(ap, v, w, eng='dve'):
        R.op(eng, lambda e: e.memset(ap, v), (), w)

    def dma(q, out, in_, r, w, chan):
        R.op(q, lambda e: e.dma_start(out=out, in_=in_), r, w, chan)

    class Pool:
        def __init__(self, banks):
            self.banks, self.i = list(banks), 0

        def get(self, n=1):
            out = []
            for _ in range(n):
                out.append(self.banks[self.i % len(self.banks)])
                self.i += 1
            return out

    dbg_n = [0]

    def dbg(name, ap, keys):
        if not debug:
            return
        dbg_n[0] += 1
        o = nc.dram_tensor("dbg_" + name, list(ap.shape), F32, kind="ExternalOutput").ap()
        dk = ('dbgout', name)
        dma('pool', o, ap, list(keys), [dk], f"dbg{dbg_n[0]}")
        R.barrier()

    wslot = [0]

    ring3 = [False]

    def wload(src3, a, b, extra_r=()):
        nsl = 3 if ring3[0] else NSLOT
        s = wslot[0] % nsl
        wslot[0] += 1
        assert a * b <= 4096
        t_ = WS[s] if s < NSLOT else GSW
        view = t_[:, 0:a * b].rearrange("p (a b) -> p a b", a=a)
        dma('pool', view, src3, extra_r, [('WS', s)], f"ws{s}")
        return view, ('WS', s)

    def kview(w2d):
        return w2d.rearrange("(kc p) n -> p kc n", p=128)

    mset(onesb[:], 1.0, [('onesb',)])
    mset(onesf[:], 1.0, [('onesf',)])
    mset(epsb[:], EPS, [('epsb',)])
    mset(negpi[:], -PI, [('negpi',)])
    dma('sp', identf[:], ident_d[:, :], (), [('identf',)], "m_ident")
    cp(identb[:], identf[:], [('identf',)], [('identb',)])
    dma('sp', vecs[:], vecs_d[:, :, :], (), [('vecs',)], "m_vecs")
    dma('sp', cTt[:], cT[:, :, :], (), [('cTt',)], "m_ct")
    act(s2[:], cTt[:], AF.Silu, [('cTt',)], [('s2',)])

    gslot = [0]

    def gload(src3):
        s = gslot[0] % 2
        gslot[0] += 1
        view = GS[s][:, 0:2048].rearrange("p (a b) -> p a b", a=16)
        dma('pool', view, src3, (), [('GS', s)], f"gs{s}")
        return view, ('GS', s)

    pmod = Pool([0, 1])
    for l in range(n_layers):
        bk = pmod.get(1)[0]
        wm = kview(w_mod[l])
        for nb4 in range(24):
            wv, wk = wload(wm[:, 0:8, nb4 * 512:(nb4 + 1) * 512], 8, 512)
            wv2, wk2 = wload(wm[:, 8:16, nb4 * 512:(nb4 + 1) * 512], 8, 512)
            for j in range(4):
                nb = nb4 * 4 + j
                for kc in range(16):
                    v, k_ = (wv, wk) if kc < 8 else (wv2, wk2)
                    mm(PS(bk)[:, 2 * nb:2 * nb + 2], v[:, kc % 8, j * 128:(j + 1) * 128], s2[:, kc, :],
                       kc == 0, kc == 15, [k_, ('s2',)], [PK(bk)])
        for pth in range(2):
            tt(mods[:, l, :, pth], PS(bk)[:, 0:192].rearrange("p (n t) -> p n t", t=2)[:, :, pth],
               vecs[:, l, V_BMOD:V_BMOD + 96], ALU.add, [PK(bk), ('vecs',)], [('mods', l)])

    def rms_rstd(chunks, nfeat, ntok, acc_banks, sq_tiles, rstd_ap, tmp_ap, tmp_key):
        nn = ntok // 512
        for i, (a, k) in enumerate(chunks):
            sq, sqk = sq_tiles[i % len(sq_tiles)]
            act(sq[:, :ntok], a, AF.Square, list(k), [sqk])
            for n_ in range(nn):
                mm(PS(acc_banks[n_]), onesb[:, :], sq[:, n_ * 512:(n_ + 1) * 512], i == 0, i == len(chunks) - 1,
                   [('onesb',), sqk], [PK(acc_banks[n_])])
        for n_ in range(nn):
            sl = slice(n_ * 512, (n_ + 1) * 512)
            act(tmp_ap[:, sl], PS(acc_banks[n_]), AF.Sqrt, [PK(acc_banks[n_]), ('epsb',)], [tmp_key],
                bias=epsb[:, 0:1], scale=1.0 / nfeat)
            recip(rstd_ap[:, sl], tmp_ap[:, sl], [tmp_key], [('rstd', n_)])

    RK = [('rstd', 0), ('rstd', 1)]

    for g in groups:
        lat = (g == 1)
        nseq, Ls = (1, 1024) if lat else (4, 256)
        NK = 1536 if lat else 1024
        NKT = NK // 128
        XBK = [('XB', c) for c in range(16)]
        dma('sp', XB[:], xT[g].rearrange("(c p) n -> p c n", p=128), (), XBK, "xin")

        for l in range(n_layers):
            def mv(j):
                return mods[:, l, j * 16:(j + 1) * 16, g]
            for half, (vg_pre, vg_post) in enumerate(((V_GPRE, V_GPOST), (V_GMPRE, V_GMPOST))):
                jb = half * 3
                stt(mA[:, g, jb + 0, :], mv(jb + 1), 1.0, vecs[:, l, vg_pre:vg_pre + 16], ALU.add, ALU.mult,
                    [('mods', l), ('vecs',)], [('mA', g, jb)])
                cp(mA[:, g, jb + 1, :], mv(jb + 0), [('mods', l)], [('mA', g, jb + 1)])
                tt(mA[:, g, jb + 2, :], mv(jb + 2), vecs[:, l, vg_post:vg_post + 16], ALU.mult,
                   [('mods', l), ('vecs',)], [('mA', g, jb + 2)])

            R.barrier()
            AR.reset()
            rstd = AR.alloc([128, NT], F32)
            sqt = [(AR.alloc([128, NT], BF16), ('sq', i)) for i in range(2)]
            xt_ = [(AR.alloc([128, NT], F32), ('xt', i)) for i in range(2)]
            rtmp, rtk = xt_[1]
            a_mark = AR.off
            rms_rstd([(XB[:, c, :], [('XB', c)]) for c in range(16)], D, NT, [0, 1], sqt, rstd, rtmp, rtk)
            for c in range(16):
                t_, tk = xt_[c % 2]
                tt(t_[:], XB[:, c, :], rstd[:], ALU.mult, [('XB', c)] + RK, [tk])
                act(H[:, c, :], t_[:], AF.Identity, [tk, ('mA', g, 0), ('mA', g, 1)], [('H', c)],
                    bias=mA[:, g, 1, c:c + 1], scale=mA[:, g, 0, c:c + 1])
            dma('sp', xs_d.rearrange("(c p) n -> p c n", p=128), XB[:], XBK + [('ARX',)] + [('MIX', c) for c in range(16)],
                [('xs',)], "xsp")
            win_v = kview(w_in[l])

            def branch_out(wmat_l, acts, gate_col, first, sig, gtmp):
                nk = len(acts)
                per = 4096 // (nk * 128)
                pb = Pool([0, 1, 2, 3, 4, 5, 6, 7])
                wmv = kview(wmat_l)
                for oc0 in range(0, 16, per):
                    wv, wk = wload(wmv[:, :, oc0 * 128:(oc0 + per) * 128], nk, per * 128)
                    for j in range(per):
                        oc = oc0 + j
                        gv, gk = gload(win_v[:, :, gate_col + oc * 128: gate_col + (oc + 1) * 128])
                        ba = pb.get(2)
                        bg = pb.get(2)
                        for kc in range(16):
                            for n_ in range(2):
                                mm(PS(bg[n_]), gv[:, kc, :], H[:, kc, n_ * 512:(n_ + 1) * 512],
                                   kc == 0, kc == 15, [gk, ('H', kc)], [PK(bg[n_])])
                        for kc in range(nk):
                            a, ak = acts[kc]
                            for n_ in range(2):
                                mm(PS(ba[n_]), wv[:, kc, j * 128:(j + 1) * 128], a[:, n_ * 512:(n_ + 1) * 512],
                                   kc == 0, kc == nk - 1, [wk, ('ARX',)] + list(ak), [PK(ba[n_])])
                        for n_ in range(2):
                            sl = slice(n_ * 512, (n_ + 1) * 512)
                            sg, sgk = sig[n_]
                            act(sg[:], PS(bg[n_]), AF.Sigmoid, [PK(bg[n_])], [sgk])
                            if first:
                                tt(MIX[:, oc, sl], PS(ba[n_]), sg[:], ALU.mult, [PK(ba[n_]), sgk], [('MIX', oc)])
                            else:
                                gt, gtk = gtmp[n_]
                                tt(gt[:], PS(ba[n_]), sg[:], ALU.mult, [PK(ba[n_]), sgk], [gtk])
                                tt(MIX[:, oc, sl], MIX[:, oc, sl], gt[:], ALU.add, [('MIX', oc), gtk], [('MIX', oc)])

            R.barrier()
            AR.reset(a_mark)
            ARX.reset()
            oall = ARX.alloc([128, 16, NT], BF16)
            raw = nc.alloc_sbuf_tensor_at(f"raw{g}_{l}", [128, 4, NT], F32, offset=ARX.base)
            cqn = AR.alloc([128, 4, NT], BF16)
            ckvk = AR.alloc([128, 4, NK], BF16)
            kpek = AR.alloc([128, NK], BF16)
            PT = [(AR.alloc([128, 512], BF16), ('PT', i)) for i in range(3)]
            rden = AR.alloc([128, 512], F32)
            m2 = AR.off
            qn = AR.alloc([128, NT], BF16)
            qr = AR.alloc([128, NT], BF16)
            Kh = AR.alloc([128, NK], BF16)
            Vh = AR.alloc([128, NKT, 128], BF16)
            if lat:
                ropet = AR.alloc([64, 2, NT], F32)
                dma('sp', ropet[:], rope_d.rearrange("t p n -> p t n"), (), [('rope',)], "m_rope")
                rt1 = AR.alloc([64, 512], F32)
                rt2 = AR.alloc([64, 512], F32)
            pp = Pool([2, 3, 4, 5, 6, 7])
            if lat:
                dma('pool', ckvk[:, :, 1024:1536], ckvc[l].rearrange("(kc p) n -> p kc n", p=128), (),
                    [('ckvk', 'c')], "cache0")
                dma('pool', kpek[:64, 1024:1536], kpec[l], (), [('kpek', 'c')], "cache1")

            for which, (col0, vg) in enumerate(((C_CQ, V_GQ), (C_CKV, V_GKV))):
                for mt2 in range(2):
                    wv, wk = wload(win_v[:, :, col0 + mt2 * 256: col0 + (mt2 + 1) * 256], 16, 256)
                    for j in range(2):
                        mt = mt2 * 2 + j
                        bks = pp.get(2)
                        for kc in range(16):
                            for n_ in range(2):
                                mm(PS(bks[n_]), wv[:, kc, j * 128:(j + 1) * 128], H[:, kc, n_ * 512:(n_ + 1) * 512],
                                   kc == 0, kc == 15, [wk, ('H', kc)], [PK(bks[n_])])
                        for n_ in range(2):
                            cp(raw[:, mt, n_ * 512:(n_ + 1) * 512], PS(bks[n_]), [PK(bks[n_]), ('ARX',)], [('raw', mt, n_), ('ARX',)],
                               eng='act' if n_ else 'dve')
                rms_rstd([(raw[:, mt, :], [('raw', mt, 0), ('raw', mt, 1)]) for mt in range(4)], 512, NT, [0, 1], sqt, rstd, rtmp, rtk)
                for mt in range(4):
                    t_, tk = xt_[0]
                    tt(t_[:], raw[:, mt, :], rstd[:], ALU.mult, [('raw', mt, 0), ('raw', mt, 1)] + RK, [tk])
                    if which == 0:
                        act(cqn[:, mt, :], t_[:], AF.Copy, [tk, ('vecs',)], [('cqn', mt)], scale=vecs[:, l, vg + mt:vg + mt + 1])
                    else:
                        ts(t_[:], t_[:], vecs[:, l, vg + mt:vg + mt + 1], None, ALU.mult, None, [tk, ('vecs',)], [tk])
                        cp(ckvk[:, mt, 0:NT], t_[:], [tk], [('ckvk', mt)], eng='act')
                        if not lat:
                            dma('sp', ckvT[l, mt * 128:(mt + 1) * 128, :], t_[:], [tk], [('ckvT', l, mt)], "ckvo")
            wv, wk = wload(win_v[:, :, C_KPE:C_KPE + 64], 16, 64)
            bks = pp.get(2)
            for kc in range(16):
                for n_ in range(2):
                    mm(PS(bks[n_])[:64, :], wv[:, kc, :], H[:, kc, n_ * 512:(n_ + 1) * 512], kc == 0, kc == 15,
                       [wk, ('H', kc)], [PK(bks[n_])])
            if not lat:
                t_, tk = xt_[0]
                for n_ in range(2):
                    cp(t_[:64, n_ * 512:(n_ + 1) * 512], PS(bks[n_])[:64, :], [PK(bks[n_])], [tk])
                cp(kpek[:64, 0:NT], t_[:64, :], [tk], [('kpek', 'o')], eng='act')
                dma('sp', kpeT[l, :, :], t_[:64, :], [tk], [('kpeT', l)], "kpeo")
            else:
                wv2, wk2 = wload(kview(w_ksw[l]), 16, 64)
                bks2 = pp.get(2)
                for kc in range(16):
                    for n_ in range(2):
                        mm(PS(bks2[n_])[:64, :], wv2[:, kc, :], H[:, kc, n_ * 512:(n_ + 1) * 512], kc == 0, kc == 15,
                           [wk2, ('H', kc)], [PK(bks2[n_])])
                for n_ in range(2):
                    sl = slice(n_ * 512, (n_ + 1) * 512)
                    tt(rt1[:], PS(bks[n_])[:64, :], ropet[:, 0, sl], ALU.mult, [PK(bks[n_]), ('rope',)], [('rt1',)])
                    tt(rt2[:], PS(bks2[n_])[:64, :], ropet[:, 1, sl], ALU.mult, [PK(bks2[n_]), ('rope',)], [('rt2',)])
                    tt(kpek[:64, sl], rt1[:], rt2[:], ALU.add, [('rt1',), ('rt2',)], [('kpek', 'o', n_)])
            kpek_keys = [('kpek', 'o')] if not lat else [('kpek', 'o', 0), ('kpek', 'o', 1), ('kpek', 'c')]
            ckvk_keys = [('ckvk', mt) for mt in range(4)] + ([('ckvk', 'c')] if lat else [])
            sc = 1.0 / math.sqrt(192.0)

            pq = Pool([2, 3, 4, 5, 6, 7])
            oacc = Pool([0, 1])
            for hd in range(NH):
                wh, whk = wload(w_head[l, hd].rearrange("(kc p) n -> p kc n", p=128), 4, 512)
                bks = pq.get(2)
                for kc in range(4):
                    for n_ in range(2):
                        mm(PS(bks[n_]), wh[:, kc, 0:128], cqn[:, kc, n_ * 512:(n_ + 1) * 512], kc == 0, kc == 3,
                           [whk, ('cqn', kc)], [PK(bks[n_])])
                for n_ in range(2):
                    cp(qn[:, n_ * 512:(n_ + 1) * 512], PS(bks[n_]), [PK(bks[n_])], [('qn', n_)], eng='act')
                bks = pq.get(2)
                for kc in range(4):
                    for n_ in range(2):
                        mm(PS(bks[n_])[:64, :], wh[:, kc, 128:192], cqn[:, kc, n_ * 512:(n_ + 1) * 512], kc == 0, kc == 3,
                           [whk, ('cqn', kc)], [PK(bks[n_])])
                if not lat:
                    for n_ in range(2):
                        cp(qr[:64, n_ * 512:(n_ + 1) * 512], PS(bks[n_])[:64, :], [PK(bks[n_])], [('qr', n_)], eng='act')
                else:
                    bks2 = pq.get(2)
                    for kc in range(4):
                        for n_ in range(2):
                            mm(PS(bks2[n_])[:64, :], wh[:, kc, 192:256], cqn[:, kc, n_ * 512:(n_ + 1) * 512], kc == 0, kc == 3,
                               [whk, ('cqn', kc)], [PK(bks2[n_])])
                    for n_ in range(2):
                        sl = slice(n_ * 512, (n_ + 1) * 512)
                        tt(rt1[:], PS(bks[n_])[:64, :], ropet[:, 0, sl], ALU.mult, [PK(bks[n_]), ('rope',)], [('rt1',)])
                        tt(rt2[:], PS(bks2[n_])[:64, :], ropet[:, 1, sl], ALU.mult, [PK(bks2[n_]), ('rope',)], [('rt2',)])
                        tt(qr[:64, sl], rt1[:], rt2[:], ALU.add, [('rt1',), ('rt2',)], [('qr', n_)])
                nkb = NK // 512
                bks = pq.get(nkb)
                for kc in range(4):
                    for n_ in range(nkb):
                        mm(PS(bks[n_]), wh[:, kc, 256:384], ckvk[:, kc, n_ * 512:(n_ + 1) * 512], kc == 0, kc == 3,
                           [whk] + ckvk_keys, [PK(bks[n_])])
                for n_ in range(nkb):
                    cp(Kh[:, n_ * 512:(n_ + 1) * 512], PS(bks[n_]), [PK(bks[n_])], [('Kh', n_)], eng='act' if n_ % 2 else 'dve')
                for kt4 in range(NKT // 4):
                    bk = pq.get(1)[0]
                    for j in range(4):
                        kt = kt4 * 4 + j
                        for kc in range(4):
                            mm(PS(bk)[:, j * 128:(j + 1) * 128], ckvk[:, kc, kt * 128:(kt + 1) * 128], wh[:, kc, 384:512],
                               kc == 0, kc == 3, [whk] + ckvk_keys, [PK(bk)])
                    cp(Vh[:, kt4 * 4:(kt4 + 1) * 4, :], PS(bk).rearrange("p (a b) -> p a b", a=4), [PK(bk)], [('Vh', kt4)],
                       eng='act' if kt4 % 2 else 'dve')
                if lat:
                    qblocks = [(qt * 512, 512, list(range(NKT))) for qt in range(2)]
                else:
                    qblocks = [(s_ * 256, 256, [2 * s_, 2 * s_ + 1]) for s_ in range(4)]
                for (q0, QN, kts) in qblocks:
                    ob = oacc.get(1)[0]
                    db = pq.get(1)[0]
                    qkeys = [('qn', q0 // 512), ('qr', q0 // 512)]
                    for i, kt in enumerate(kts):
                        sb = pq.get(1)[0]
                        if sb == db:
                            sb = pq.get(1)[0]
                        mm(PS(sb)[:, :QN], Kh[:, kt * 128:(kt + 1) * 128], qn[:, q0:q0 + QN], True, False,
                           [('Kh', kt // 4), qkeys[0]], [PK(sb)])
                        mm(PS(sb)[:, :QN], kpek[:64, kt * 128:(kt + 1) * 128], qr[:64, q0:q0 + QN], False, True,
                           kpek_keys + [qkeys[1]], [PK(sb)])
                        pt, ptk = PT[i % 3]
                        act(pt[:, :QN], PS(sb)[:, :QN], AF.Exp, [PK(sb)], [ptk], scale=sc)
                        mm(PS(ob)[:, :QN], Vh[:, kt, :], pt[:, :QN], i == 0, i == len(kts) - 1, [('Vh', kt // 4), ptk], [PK(ob)])
                        mm(PS(db)[:, :QN], onesb[:, :], pt[:, :QN], i == 0, i == len(kts) - 1, [('onesb',), ptk], [PK(db)])
                    recip(rden[:, :QN], PS(db)[:, :QN], [PK(db)], [('rden',)])
                    tt(oall[:, hd, q0:q0 + QN], PS(ob)[:, :QN], rden[:, :QN], ALU.mult, [PK(ob), ('rden',), ('ARX',)],
                       [('oall', hd, q0 // 512), ('ARX',)])
            R.barrier()
            AR.reset(m2)
            sig = [(AR.alloc([128, 512], F32), ('sig', i)) for i in range(2)]
            gtmp = [(AR.alloc([128, 512], F32), ('gtmp', i)) for i in range(2)]
            if l == 0:
                dbg(f"oall{g}", oall[:, 0:2, :], [('ARX',)])
            branch_out(w_ao[l], [(oall[:, hd, :], [('oall', hd, 0), ('oall', hd, 1)]) for hd in range(NH)], C_GA, True, sig, gtmp)

            if l == 0:
                dbg(f"mixa{g}", MIX[:, 0:2, :], [('MIX', 0), ('MIX', 1)])
            R.barrier()
            AR.reset(a_mark)
            ARX.reset()
            ycv = ARX.alloc([128, 8, NT], F32)
            sig = [(AR.alloc([128, 512], F32), ('sig', i)) for i in range(2)]
            gtmp = [(AR.alloc([128, 512], F32), ('gtmp', i)) for i in range(2)]
            PADW = Ls + 30
            zpad = [(AR.alloc([128, nseq, PADW], BF16), ('zpad', i)) for i in range(2)]
            diag = AR.alloc([128, 31, 128], BF16)
            cact = AR.alloc([128, 8, NT], BF16)
            sgb = AR.alloc([128, NT], F32)
            mu = AR.alloc([128, NT], F32)
            for zp, zk in zpad:
                mset(zp[:], 0.0, [zk], eng='pool')
            pc = Pool([0, 1, 2, 3, 4, 5, 6, 7])
            for ct in range(8):
                wa, wak = wload(win_v[:, :, C_CVA + ct * 128:C_CVA + (ct + 1) * 128], 16, 128)
                wb, wbk = wload(win_v[:, :, C_CVB + ct * 128:C_CVB + (ct + 1) * 128], 16, 128)
                bb = pc.get(2)
                ba = pc.get(2)
                for kc in range(16):
                    for n_ in range(2):
                        mm(PS(bb[n_]), wb[:, kc, :], H[:, kc, n_ * 512:(n_ + 1) * 512], kc == 0, kc == 15, [wbk, ('H', kc)], [PK(bb[n_])])
                for kc in range(16):
                    for n_ in range(2):
                        mm(PS(ba[n_]), wa[:, kc, :], H[:, kc, n_ * 512:(n_ + 1) * 512], kc == 0, kc == 15, [wak, ('H', kc)], [PK(ba[n_])])
                zp, zk = zpad[ct % 2]
                for n_ in range(2):
                    act(sgb[:, n_ * 512:(n_ + 1) * 512], PS(bb[n_]), AF.Sigmoid, [PK(bb[n_])], [('sgb', n_)])
                    if Ls >= 512:
                        tt(zp[:, 0, 15 + n_ * 512: 15 + (n_ + 1) * 512], PS(ba[n_]), sgb[:, n_ * 512:(n_ + 1) * 512], ALU.mult,
                           [PK(ba[n_]), ('sgb', n_)], [zk])
                    else:
                        spb = 512 // Ls
                        tt(zp[:, n_ * spb:(n_ + 1) * spb, 15:15 + Ls], PS(ba[n_]).rearrange("p (s t) -> p s t", s=spb),
                           sgb[:, n_ * 512:(n_ + 1) * 512].rearrange("p (s t) -> p s t", s=spb), ALU.mult,
                           [PK(ba[n_]), ('sgb', n_)], [zk])
                for j in range(31):
                    cwj = vecs[:, l, V_CW + ct * 31 + j: V_CW + ct * 31 + j + 1]
                    if j % 2:
                        act(diag[:, j, :], identb[:], AF.Copy, [('identb',), ('vecs',)], [('diag', j)], scale=cwj)
                    else:
                        ts(diag[:, j, :], identb[:], cwj, None, ALU.mult, None, [('identb',), ('vecs',)], [('diag', j)])
                bo = pc.get(2)
                if Ls >= 512:
                    for j in range(31):
                        for n_ in range(2):
                            mm(PS(bo[n_]), diag[:, j, :], zp[:, 0, j + n_ * 512: j + (n_ + 1) * 512], j == 0, j == 30,
                               [('diag', j), zk], [PK(bo[n_])])
                else:
                    for s_ in range(nseq):
                        b_ = bo[s_ // 2]
                        for j in range(31):
                            mm(PS(b_)[:, (s_ % 2) * Ls:(s_ % 2 + 1) * Ls], diag[:, j, :], zp[:, s_, j:j + Ls], j == 0, j == 30,
                               [('diag', j), zk], [PK(b_)])
                for n_ in range(2):
                    act(ycv[:, ct, n_ * 512:(n_ + 1) * 512], PS(bo[n_]), AF.Identity, [PK(bo[n_]), ('vecs',), ('ARX',)],
                        [('ycv', ct, n_), ('ARX',)], bias=vecs[:, l, V_CB + ct:V_CB + ct + 1])
            for ct in range(8):
                sq, sqk = sqt[0]
                yb, ybk = sqt[1]
                yk = [('ycv', ct, 0), ('ycv', ct, 1), ('ARX',)]
                act(sq[:], ycv[:, ct, :], AF.Square, yk, [sqk])
                cp(yb[:], ycv[:, ct, :], yk, [ybk])
                for n_ in range(2):
                    mm(PS(n_), onesb[:, :], yb[:, n_ * 512:(n_ + 1) * 512], ct == 0, ct == 7, [('onesb',), ybk], [PK(n_)])
                    mm(PS(2 + n_), onesb[:, :], sq[:, n_ * 512:(n_ + 1) * 512], ct == 0, ct == 7, [('onesb',), sqk], [PK(2 + n_)])
            for n_ in range(2):
                sl = slice(n_ * 512, (n_ + 1) * 512)
                act(mu[:, sl], PS(n_), AF.Copy, [PK(n_)], [('mu', n_)], scale=1.0 / 1024)
                tt(rtmp[:, sl], mu[:, sl], mu[:, sl], ALU.mult, [('mu', n_)], [rtk])
                stt(rtmp[:, sl], PS(2 + n_), 1.0 / 1024, rtmp[:, sl], ALU.mult, ALU.subtract, [PK(2 + n_), rtk], [rtk])
                act(rtmp[:, sl], rtmp[:, sl], AF.Sqrt, [rtk, ('epsb',)], [rtk], bias=epsb[:, 0:1])
                recip(rstd[:, sl], rtmp[:, sl], [rtk], [('rstd', n_)])
            for ct in range(8):
                t_, tk = xt_[0]
                yk = [('ycv', ct, 0), ('ycv', ct, 1), ('ARX',)]
                tt(t_[:], ycv[:, ct, :], mu[:], ALU.subtract, yk + [('mu', 0), ('mu', 1)], [tk])
                tt(t_[:], t_[:], rstd[:], ALU.mult, [tk] + RK, [tk])
                act(cact[:, ct, :], t_[:], AF.Silu, [tk, ('vecs',)], [('cact', ct)],
                    bias=vecs[:, l, V_LNB + ct:V_LNB + ct + 1], scale=vecs[:, l, V_LNG + ct:V_LNG + ct + 1])
            if l == 0:
                dbg(f"cact{g}", cact[:, 0:2, :], [('cact', 0), ('cact', 1)])
            branch_out(w_co[l], [(cact[:, ct, :], [('cact', ct)]) for ct in range(8)], C_GC, False, sig, gtmp)

            if l == 0:
                dbg(f"mixc{g}", MIX[:, 0:2, :], [('MIX', 0), ('MIX', 1)])
            R.barrier()
            AR.reset(0)
            ARX.reset()
            ub = ARX.alloc([128, 8, NT], BF16)
            gb = ARX.alloc([128, 8, NT], BF16)
            prm = AR.alloc([128, 24, 64], F32)
            LPr = AR.alloc([128, 9, 64], F32)
            LPi = AR.alloc([128, 9, 64], F32)
            LPn = AR.alloc([128, 9, 64], F32)
            Qr = AR.alloc([128, 7, 64], F32)
            Qi = AR.alloc([128, 7, 64], F32)
            Qn = AR.alloc([128, 7, 64], F32)
            c0 = AR.alloc([128, 2, 64], F32)
            Bex = [[(AR.alloc([128, 512], BF16), ('Bex', d_, ri)) for ri in range(2)] for d_ in range(2)]
            bxt = [(AR.alloc([128, 512], F32), ('bxt', i)) for i in range(2)]
            btm = [(AR.alloc([128, 512], F32), ('btm', i)) for i in range(2)]
            dgf = [(AR.alloc([128, 128], F32), ('dgf', i)) for i in range(2)]
            dgD = AR.alloc([128, 128], BF16)
            NB = Ls // 8
            PADE = 64 if lat else 16
            EW = PADE + NB
            NSET = 4
            Ebuf = [[AR.alloc([128, 2, nseq, EW], F32) for pp_ in range(2)] for s_ in range(NSET)]
            Cex = [[(AR.alloc([128, 128], BF16), ('Cex', s_, ri)) for ri in range(2)] for s_ in range(NSET)]
            dgP = [nc.alloc_sbuf_tensor_at(f"dgP0_{g}_{l}", [128, 8, 3, 128], BF16, offset=gs_off), AR.alloc([128, 8, 3, 128], BF16)]
            m3 = AR.off
            xs4 = [(AR.alloc([128, 2, NT], BF16), ('xs4', s_)) for s_ in range(NSET)]
            for s_ in range(NSET):
                for pp_ in range(2):
                    mset(Ebuf[s_][pp_][:], 0.0, [('E', s_, pp_)], eng='pool')

            dma('sp', prm[:, 0:3, :], ssm_s[l].rearrange("t p n -> p t n"), (), [('prm',)], "m_prm")
            P_ = lambda i: prm[:, i, :]
            pk = [('prm',)]
            are, aim, ldt = P_(0), P_(1), P_(2)
            fact = [1.0 / math.factorial(k) for k in range(12)]
            ts(P_(3), ldt, 0.125, None, ALU.mult, None, pk, pk)
            ts(P_(4), P_(3), fact[10], None, ALU.mult, None, pk, pk)
            for k in range(9, 0, -1):
                stt(P_(4), P_(4), fact[k], P_(3), ALU.add, ALU.mult, pk, pk)
            ts(P_(4), P_(4), 1.0, None, ALU.add, None, pk, pk)
            for _ in range(3):
                tt(P_(4), P_(4), P_(4), ALU.mult, pk, pk)
            tt(P_(5), are, P_(4), ALU.mult, pk, pk)
            tt(P_(6), aim, P_(4), ALU.mult, pk, pk)
            ts(P_(7), P_(5), fact[6], None, ALU.mult, None, pk, pk)
            for k in range(5, 0, -1):
                stt(P_(7), P_(7), fact[k], P_(5), ALU.add, ALU.mult, pk, pk)
            ts(P_(7), P_(7), 1.0, None, ALU.add, None, pk, pk)
            act(P_(8), P_(6), AF.Sin, pk, pk, scale=0.125)
            act(P_(9), P_(6), AF.Sin, pk, pk, scale=0.0625)
            tt(P_(9), P_(9), P_(9), ALU.mult, pk, pk)
            ts(P_(9), P_(9), -2.0, 1.0, ALU.mult, ALU.add, pk, pk)
            for _ in range(3):
                tt(P_(16), P_(8), P_(9), ALU.mult, pk, pk)
                tt(P_(17), P_(9), P_(9), ALU.mult, pk, pk)
                tt(P_(18), P_(8), P_(8), ALU.mult, pk, pk)
                tt(P_(9), P_(17), P_(18), ALU.subtract, pk, pk)
                ts(P_(8), P_(16), 2.0, None, ALU.mult, None, pk, pk)
            tt(LPr[:, 1, :], P_(7), P_(9), ALU.mult, pk, [('LP',)])
            tt(LPi[:, 1, :], P_(7), P_(8), ALU.mult, pk, [('LP',)])
            tt(P_(10), are, are, ALU.mult, pk, pk)
            tt(P_(11), aim, aim, ALU.mult, pk, pk)
            tt(P_(10), P_(10), P_(11), ALU.add, pk, pk)
            recip(P_(10), P_(10), pk, pk)
            ts(P_(11), LPr[:, 1, :], -1.0, None, ALU.add, None, [('LP',)], pk)
            tt(P_(12), P_(11), are, ALU.mult, pk, pk)
            tt(P_(13), LPi[:, 1, :], aim, ALU.mult, pk + [('LP',)], pk)
            tt(P_(12), P_(12), P_(13), ALU.add, pk, pk)
            tt(P_(14), P_(12), P_(10), ALU.mult, pk, pk)
            tt(P_(12), LPi[:, 1, :], are, ALU.mult, pk + [('LP',)], pk)
            tt(P_(13), P_(11), aim, ALU.mult, pk, pk)
            tt(P_(12), P_(12), P_(13), ALU.subtract, pk, pk)
            tt(P_(15), P_(12), P_(10), ALU.mult, pk, pk)
            fre, fim = P_(14), P_(15)
            for k in range(2, 9):
                tt(P_(16), LPr[:, k - 1, :], LPr[:, 1, :], ALU.mult, [('LP',)], pk)
                tt(P_(17), LPi[:, k - 1, :], LPi[:, 1, :], ALU.mult, [('LP',)], pk)
                tt(LPr[:, k, :], P_(16), P_(17), ALU.subtract, pk, [('LP',)])
                tt(P_(16), LPr[:, k - 1, :], LPi[:, 1, :], ALU.mult, [('LP',)], pk)
                tt(P_(17), LPi[:, k - 1, :], LPr[:, 1, :], ALU.mult, [('LP',)], pk)
                tt(LPi[:, k, :], P_(16), P_(17), ALU.add, pk, [('LP',)])
            ts(LPn[:, 1:9, :], LPi[:, 1:9, :], -1.0, None, ALU.mult, None, [('LP',)], [('LPn',)])
            cp(Qr[:, 0, :], LPr[:, 8, :], [('LP',)], [('Q',)])
            cp(Qi[:, 0, :], LPi[:, 8, :], [('LP',)], [('Q',)])
            for k in range(1, 7):
                tt(P_(16), Qr[:, k - 1, :], Qr[:, k - 1, :], ALU.mult, [('Q',)], pk)
                tt(P_(17), Qi[:, k - 1, :], Qi[:, k - 1, :], ALU.mult, [('Q',)], pk)
                tt(Qr[:, k, :], P_(16), P_(17), ALU.subtract, pk, [('Q',)])
                tt(P_(16), Qr[:, k - 1, :], Qi[:, k - 1, :], ALU.mult, [('Q',)], pk)
                ts(Qi[:, k, :], P_(16), 2.0, None, ALU.mult, None, pk, [('Q',)])
            ts(Qn[:], Qi[:], -1.0, None, ALU.mult, None, [('Q',)], [('Qn',)])
            if lat:
                dma('sp', prm[:, 22:24, :], h0d[l].rearrange("t p n -> p t n"), (), [('h0',)], "m_h0")
                hk = [('h0',), ('LP',)]
                tt(P_(20), LPr[:, 8, :], P_(22), ALU.mult, hk, [('h0t',)])
                tt(P_(21), LPi[:, 8, :], P_(23), ALU.mult, hk, [('h0t',)])
                tt(c0[:, 0, :], P_(20), P_(21), ALU.subtract, [('h0t',)], [('c0',)])
                tt(P_(20), LPr[:, 8, :], P_(23), ALU.mult, hk, [('h0t',)])
                tt(P_(21), LPi[:, 8, :], P_(22), ALU.mult, hk, [('h0t',)])
                tt(c0[:, 1, :], P_(20), P_(21), ALU.add, [('h0t',)], [('c0',)])

            if l == 0:
                dbg(f"prm{g}", prm[:], pk + [('h0',), ('h0t',)] if lat else pk)
                dbg(f"LPr{g}", LPr[:, 1:9, :], [('LP',)])
                dbg(f"LPi{g}", LPi[:, 1:9, :], [('LP',)])
                dbg(f"Qr{g}", Qr[:], [('Q',)])
            pu = Pool([4, 5, 6, 7])
            for ct in range(8):
                wv, wk = wload(win_v[:, :, C_SSM + ct * 128:C_SSM + (ct + 1) * 128], 16, 128)
                bks = pu.get(2)
                for kc in range(16):
                    for n_ in range(2):
                        mm(PS(bks[n_]), wv[:, kc, :], H[:, kc, n_ * 512:(n_ + 1) * 512], kc == 0, kc == 15, [wk, ('H', kc)], [PK(bks[n_])])
                for n_ in range(2):
                    cp(ub[:, ct, n_ * 512:(n_ + 1) * 512], PS(bks[n_]), [PK(bks[n_]), ('ARX',)], [('ub', ct, n_), ('ARX',)], eng='act')

            def scan_gen(XS, xk, st_, d_, col):
                eng = 'dve'
                doff = PADE if d_ == 0 else 0
                X5 = XS[:].rearrange("p r (s a b) -> p r s a b", s=nseq, b=8)
                lk = [('LP',), ('LPn',)]
                Eb = Ebuf[st_]
                ek = lambda p_: [('E', st_, p_)]

                def cstep(dst, srcv, acc, mr, mi, mn, rk, wk_):
                    stt(dst, srcv, mr, acc, ALU.mult, ALU.add, rk, wk_, eng=eng)
                    yield
                    stt(dst[:, 0], srcv[:, 1], mn, dst[:, 0], ALU.mult, ALU.add, rk, wk_, eng=eng)
                    yield
                    stt(dst[:, 1], srcv[:, 0], mi, dst[:, 1], ALU.mult, ALU.add, rk, wk_, eng=eng)
                    yield

                lam = lambda k: (LPr[:, k, col:col + 1], LPi[:, k, col:col + 1], LPn[:, k, col:col + 1])
                if lat:
                    epos = doff if d_ == 0 else doff + NB - 1
                    tt(Eb[0][:, :, 0, epos], Eb[0][:, :, 0, epos], c0[:, :, col], ALU.add, ek(0) + [('c0',)], ek(0))
                    yield
                cur, k, s_ = 0, 0, 1
                while s_ < NB:
                    A_, B_ = Eb[cur], Eb[1 - cur]
                    sh = -s_ if d_ == 0 else s_
                    yield from cstep(B_[:, :, :, doff:doff + NB], A_[:, :, :, doff + sh:doff + sh + NB], A_[:, :, :, doff:doff + NB],
                                     Qr[:, k, col:col + 1], Qi[:, k, col:col + 1], Qn[:, k, col:col + 1],
                                     ek(cur) + [('Q',), ('Qn',)], ek(1 - cur))
                    cur = 1 - cur
                    s_ *= 2
                    k += 1
                Ef = Eb[cur]
                ppos = doff - 1 if d_ == 0 else doff + NB
                if lat:
                    cp(Ef[:, :, 0, ppos], prm[:, 22:24, col], ek(cur) + [('h0',)], ek(cur))
                    yield
                eb = 7 if d_ == 0 else 0
                if d_ == 0:
                    for b in range(0, 7):
                        m = lam(b + 1)
                        yield from cstep(X5[:, :, :, :, b], Ef[:, :, :, doff - 1:doff - 1 + NB], X5[:, :, :, :, b], m[0], m[1], m[2],
                                         [xk] + lk + ek(cur), [xk])
                else:
                    for b in range(1, 8):
                        m = lam(8 - b)
                        yield from cstep(X5[:, :, :, :, b], Ef[:, :, :, doff + 1:doff + 1 + NB], X5[:, :, :, :, b], m[0], m[1], m[2],
                                         [xk] + lk + ek(cur), [xk])
                cp(X5[:, :, :, :, eb], Ef[:, :, :, doff:doff + NB], ek(cur), [xk], eng=eng)
                yield
                if lat:
                    mset(Ef[:, :, 0, ppos:ppos + 1], 0.0, ek(cur))
                    yield
                if not lat:
                    fa = doff + NB - 1 if d_ == 0 else doff
                    cp(FIN[:, :, :, d_, col % 32], Ef[:, :, :, fa], ek(cur), [('FIN',)])
                    yield

            ybank = [0, 1]
            pbu = Pool([2, 3, 4, 5, 6, 7])

            def bex_for(ct):
                for d_ in range(2):
                    fb = pbu.get(2)
                    for ri, fsrc in enumerate((fre, fim)):
                        for tl in range(4):
                            col = d_ * 32 + ct * 4 + tl
                            dg, dgk = dgf[(ri * 4 + tl) % 2]
                            ts(dg[:], identf[:], fsrc[:, col:col + 1], None, ALU.mult, None, [('identf',)] + pk, [dgk])
                            mm(PS(fb[ri])[:, tl * 128:(tl + 1) * 128], onesf[:, :], dg[:], True, True, [('onesf',), dgk], [PK(fb[ri])])
                    bre_t, brk = bxt[0]
                    bim_t, bik = bxt[1]
                    dma('sp', bre_t[:], b_x[l, d_, ct, 0], (), [brk], "bx0")
                    dma('sp', bim_t[:], b_x[l, d_, ct, 1], (), [bik], "bx1")
                    t1, t1k = btm[0]
                    t2, t2k = btm[1]
                    tt(t1[:], PS(fb[0]), bre_t[:], ALU.mult, [PK(fb[0]), brk], [t1k])
                    tt(t2[:], PS(fb[1]), bim_t[:], ALU.mult, [PK(fb[1]), bik], [t2k])
                    tt(Bex[d_][0][0][:], t1[:], t2[:], ALU.subtract, [t1k, t2k], [Bex[d_][0][1]])
                    tt(t1[:], PS(fb[0]), bim_t[:], ALU.mult, [PK(fb[0]), bik], [t1k])
                    tt(t2[:], PS(fb[1]), bre_t[:], ALU.mult, [PK(fb[1]), brk], [t2k])
                    tt(Bex[d_][1][0][:], t1[:], t2[:], ALU.add, [t1k, t2k], [Bex[d_][1][1]])

            tiles = [(ct, d_, tl) for ct in range(8) for d_ in range(2) for tl in range(4)]

            def pre(ti):
                ct, d_, tl = tiles[ti]
                if d_ == 0 and tl == 0:
                    bex_for(ct)
                ubk = [('ub', ct, 0), ('ub', ct, 1), ('ARX',)]
                tile_ = ct * 4 + tl
                col = d_ * 32 + tile_
                st_ = ti % NSET
                XS, xk = xs4[st_]
                (cre, crk), (cim, cik) = Cex[st_]
                dma('pool', cre[:], c_x[l, d_, tile_, 0], (), [crk], f"cx{st_}0")
                dma('pool', cim[:], c_x[l, d_, tile_, 1], (), [cik], f"cx{st_}1")
                doff = PADE if d_ == 0 else 0
                pz = slice(0, PADE) if d_ == 0 else slice(NB, NB + PADE)
                for pp_ in range(2):
                    mset(Ebuf[st_][pp_][:, :, :, pz], 0.0, [('E', st_, pp_)], eng='pool')
                br_ = pbu.get(2)
                bi_ = pbu.get(2)
                for n_ in range(2):
                    mm(PS(br_[n_]), Bex[d_][0][0][:, tl * 128:(tl + 1) * 128], ub[:, ct, n_ * 512:(n_ + 1) * 512], True, True,
                       [Bex[d_][0][1]] + ubk, [PK(br_[n_])])
                    mm(PS(bi_[n_]), Bex[d_][1][0][:, tl * 128:(tl + 1) * 128], ub[:, ct, n_ * 512:(n_ + 1) * 512], True, True,
                       [Bex[d_][1][1]] + ubk, [PK(bi_[n_])])
                for n_ in range(2):
                    cp(XS[:, 0, n_ * 512:(n_ + 1) * 512], PS(br_[n_]), [PK(br_[n_])], [xk], eng='act')
                    cp(XS[:, 1, n_ * 512:(n_ + 1) * 512], PS(bi_[n_]), [PK(bi_[n_])], [xk], eng='act')
                dg3 = dgP[ti % 2]
                dk = ('dgP', ti % 2)
                for dl in range(1, 8):
                    for j3, tab in enumerate((LPr, LPi, LPn)):
                        act(dg3[:, dl, j3, :], identb[:], AF.Copy, [('identb',), ('LP',), ('LPn',)], [dk], scale=tab[:, dl, col:col + 1])
                X5 = XS[:].rearrange("p r (s a b) -> p r s a b", s=nseq, b=8)
                pr_ = pbu.get(2)
                pi_ = pbu.get(2)
                for b in range(8):
                    srcs = list(range(0, b + 1)) if d_ == 0 else list(range(b, 8))
                    for ri, bks in ((0, pr_), (1, pi_)):
                        outv = PS(bks[b // 4])[:, (b % 4) * 128:(b % 4 + 1) * 128].rearrange("p (s a) -> p s a", s=nseq)
                        terms = []
                        for b2 in srcs:
                            dl = abs(b - b2)
                            if dl == 0:
                                terms.append((identb[:], X5[:, ri, :, :, b2], [('identb',)]))
                            elif ri == 0:
                                terms.append((dg3[:, dl, 0, :], X5[:, 0, :, :, b2], [dk]))
                                terms.append((dg3[:, dl, 2, :], X5[:, 1, :, :, b2], [dk]))
                            else:
                                terms.append((dg3[:, dl, 0, :], X5[:, 1, :, :, b2], [dk]))
                                terms.append((dg3[:, dl, 1, :], X5[:, 0, :, :, b2], [dk]))
                        for i_, (lh, rh, kk) in enumerate(terms):
                            mm(outv, lh, rh, i_ == 0, i_ == len(terms) - 1, kk + [xk], [PK(bks[b // 4])])
                X5p = XS[:].rearrange("p r (s a b) -> p r b s a", s=nseq, b=8)
                eb = 7 if d_ == 0 else 0
                for ri, bks in ((0, pr_), (1, pi_)):
                    for h_ in range(2):
                        cp(X5p[:, ri, 4 * h_:4 * h_ + 4], PS(bks[h_]).rearrange("p (b s a) -> p b s a", b=4, s=nseq),
                           [PK(bks[h_])], [xk], eng='act')
                    cp(Ebuf[st_][0][:, ri, :, doff:doff + NB],
                       PS(bks[eb // 4])[:, (eb % 4) * 128:(eb % 4 + 1) * 128].rearrange("p (s a) -> p s a", s=nseq),
                       [PK(bks[eb // 4])], [('E', st_, 0)], eng='act')

            def post(ti):
                ct, d_, tl = tiles[ti]
                st_ = ti % NSET
                XS, xk = xs4[st_]
                (cre, crk), (cim, cik) = Cex[st_]
                if d_ == 0 and tl == 0:
                    ubk = [('ub', ct, 0), ('ub', ct, 1), ('ARX',)]
                    ts(dgD[:], identb[:], vecs[:, l, V_SD + ct:V_SD + ct + 1], None, ALU.mult, None, [('identb',), ('vecs',)], [('dgD',)])
                    for n_ in range(2):
                        mm(PS(ybank[n_]), dgD[:], ub[:, ct, n_ * 512:(n_ + 1) * 512], True, False, [('dgD',)] + ubk, [PK(ybank[n_])])
                last = (d_ == 1 and tl == 3)
                act(XS[:, 1, :], XS[:, 1, :], AF.Copy, [xk], [xk], scale=-1.0)
                for n_ in range(2):
                    mm(PS(ybank[n_]), cre[:], XS[:, 0, n_ * 512:(n_ + 1) * 512], False, False, [crk, xk], [PK(ybank[n_])])
                    mm(PS(ybank[n_]), cim[:], XS[:, 1, n_ * 512:(n_ + 1) * 512], False, last, [cik, xk], [PK(ybank[n_])])
                if last:
                    for n_ in range(2):
                        act(gb[:, ct, n_ * 512:(n_ + 1) * 512], PS(ybank[n_]), AF.Gelu_apprx_tanh, [PK(ybank[n_]), ('ARX',)],
                            [('gb', ct, n_), ('ARX',)])

            def scan_pair(k):
                gens = []
                for ti in (2 * k, 2 * k + 1):
                    ct, d_, tl = tiles[ti]
                    XS, xk = xs4[ti % NSET]
                    gens.append(scan_gen(XS, xk, ti % NSET, d_, d_ * 32 + ct * 4 + tl))
                while gens:
                    for g_ in list(gens):
                        try:
                            next(g_)
                        except StopIteration:
                            gens.remove(g_)

            npair = len(tiles) // 2
            pre(0)
            pre(1)
            for k in range(npair):
                if k + 1 < npair:
                    pre(2 * k + 2)
                    pre(2 * k + 3)
                scan_pair(k)
                post(2 * k)
                post(2 * k + 1)
            if l == 0:
                dbg(f"ub{g}", ub[:, 0, :], [('ub', 0, 0), ('ub', 0, 1), ('ARX',)])
                dbg(f"Bex{g}", Bex[1][0][0][:], [Bex[1][0][1]])
                dbg(f"gb{g}", gb[:, 0, :], [('gb', 0, 0), ('gb', 0, 1), ('ARX',)])
            if not lat:
                dma('sp', fin_d[l].rearrange("t p n -> p t n"), FIN[:].rearrange("p t s d i -> p t (s d i)"), [('FIN',)], [('fin', l)], "fino")
            R.barrier()
            AR.reset(m3)
            sig = [(AR.alloc([128, 512], F32), ('sig', i)) for i in range(2)]
            gtmp = [(AR.alloc([128, 512], F32), ('gtmp', i)) for i in range(2)]
            pgl = Pool([2, 3, 4, 5, 6, 7])
            wgv = kview(w_glu[l])
            for c4 in range(2):
                wv, wk = wload(wgv[:, :, c4 * 512:(c4 + 1) * 512], 8, 512)
                for j in range(4):
                    co = c4 * 4 + j
                    bks = pgl.get(2)
                    for kc in range(8):
                        for n_ in range(2):
                            mm(PS(bks[n_]), wv[:, kc, j * 128:(j + 1) * 128], gb[:, kc, n_ * 512:(n_ + 1) * 512], kc == 0, kc == 7,
                               [wk, ('gb', kc, n_), ('ARX',)], [PK(bks[n_])])
                    for n_ in range(2):
                        sl = slice(n_ * 512, (n_ + 1) * 512)
                        sg, sgk = sig[n_]
                        act(sg[:], PS(bks[n_]), AF.Sigmoid, [PK(bks[n_]), ('vecs',)], [sgk], bias=vecs[:, l, V_BGLU + co:V_BGLU + co + 1])
                        tt(ub[:, co, sl], gb[:, co, sl], sg[:], ALU.mult, [('gb', co, n_), sgk, ('ARX',)], [('ub', co, n_), ('ARX',)])
            if l == 0:
                dbg(f"so{g}", ub[:, 0:2, :], [('ARX',)])
            branch_out(w_so[l], [(ub[:, ct, :], [('ub', ct, 0), ('ub', ct, 1)]) for ct in range(8)], C_GS, False, sig, gtmp)

            if l == 0:
                dbg(f"mixs{g}", MIX[:, 0:2, :], [('MIX', 0), ('MIX', 1)])
            R.barrier()
            AR.reset(a_mark)
            mixb = AR.alloc([128, 16, NT], BF16)
            for c in range(16):
                cp(mixb[:, c, :], MIX[:, c, :], [('MIX', c)], [('mixb', c)], eng='act' if c % 2 else 'dve')
            wov = kview(w_out[l])
            po = Pool([2, 3, 4, 5, 6, 7])
            for oc2 in range(8):
                wv, wk = wload(wov[:, :, oc2 * 256:(oc2 + 1) * 256], 16, 256)
                for j in range(2):
                    oc = oc2 * 2 + j
                    bks = po.get(2)
                    for kc in range(16):
                        for n_ in range(2):
                            mm(PS(bks[n_]), wv[:, kc, j * 128:(j + 1) * 128], mixb[:, kc, n_ * 512:(n_ + 1) * 512], kc == 0, kc == 15,
                               [wk, ('mixb', kc)], [PK(bks[n_])])
                    extra = [('ARX',)] if oc >= 8 else [('MIX', 2 * oc), ('MIX', 2 * oc + 1)]
                    for n_ in range(2):
                        cp(XB[:, oc, n_ * 512:(n_ + 1) * 512], PS(bks[n_]), [PK(bks[n_])], [('XB', oc)] + extra, eng='act' if n_ else 'dve')
            rms_rstd([(XB[:, c, :], [('XB', c)]) for c in range(16)], D, NT, [0, 1], sqt, rstd, rtmp, rtk)
            for c in range(16):
                t_, tk = xt_[0]
                dma('sp', t_[:], xs_d[c * 128:(c + 1) * 128, :], [('xs',)], [tk], "xld")
                tt(XB[:, c, :], XB[:, c, :], rstd[:], ALU.mult, [('XB', c)] + RK, [('XB', c)])
                stt(XB[:, c, :], XB[:, c, :], mA[:, g, 2, c:c + 1], t_[:], ALU.mult, ALU.add, [('XB', c), tk, ('mA', g, 2)], [('XB', c)])

            if l == 0:
                dbg(f"x1{g}", XB[:, 0:2, :], [('XB', 0), ('XB', 1)])
            R.barrier()
            AR.reset(a_mark)
            hid = AR.alloc([128, 16, 512], BF16)
            fbuf = AR.alloc([128, 16, 512], F32)
            rl = [(AR.alloc([128, 512], F32), ('rl', i)) for i in range(2)]
            rms_rstd([(XB[:, c, :], [('XB', c)]) for c in range(16)], D, NT, [0, 1], sqt, rstd, rtmp, rtk)
            for c in range(16):
                t_, tk = xt_[0]
                tt(t_[:], XB[:, c, :], rstd[:], ALU.mult, [('XB', c)] + RK, [tk])
                act(H[:, c, :], t_[:], AF.Identity, [tk, ('mA', g, 3), ('mA', g, 4)], [('H', c)],
                    bias=mA[:, g, 4, c:c + 1], scale=mA[:, g, 3, c:c + 1])
            ring3[0] = True
            f1v = kview(w_ff1[l])
            f2v = kview(w_ff2[l])
            pf = Pool([2, 3, 4, 5, 6, 7])
            for tt_i in range(2):
                tsl = slice(tt_i * 512, (tt_i + 1) * 512)
                for hq in range(4):
                    for hp in range(8):
                        c0_ = (hq * 16 + hp * 2) * 128
                        wv, wk = wload(f1v[:, :, c0_:c0_ + 256], 16, 256)
                        for j in range(2):
                            hc = hp * 2 + j
                            bk = pf.get(1)[0]
                            for kc in range(16):
                                mm(PS(bk), wv[:, kc, j * 128:(j + 1) * 128], H[:, kc, tsl], kc == 0, kc == 15, [wk, ('H', kc)], [PK(bk)])
                            r_, rk_ = rl[hc % 2]
                            act(r_[:], PS(bk), AF.Relu, [PK(bk)], [rk_])
                            tt(hid[:, hc, :], r_[:], r_[:], ALU.mult, [rk_], [('hid', hc)])
                    for oc2 in range(8):
                        wv, wk = wload(f2v[:, hq * 16:(hq + 1) * 16, oc2 * 256:(oc2 + 1) * 256], 16, 256)
                        for j in range(2):
                            oc = oc2 * 2 + j
                            bk = pf.get(1)[0]
                            for hc in range(16):
                                mm(PS(bk), wv[:, hc, j * 128:(j + 1) * 128], hid[:, hc, :], hc == 0, hc == 15, [wk, ('hid', hc)], [PK(bk)])
                            if hq == 0:
                                cp(fbuf[:, oc, :], PS(bk), [PK(bk)], [('f', oc)], eng='act')
                            else:
                                tt(fbuf[:, oc, :], fbuf[:, oc, :], PS(bk), ALU.add, [('f', oc), PK(bk)], [('f', oc)])
                rms_rstd([(fbuf[:, c, :], [('f', c)]) for c in range(16)], D, 512, [0], sqt, rstd, rtmp, rtk)
                for c in range(16):
                    tt(fbuf[:, c, :], fbuf[:, c, :], rstd[:, 0:512], ALU.mult, [('f', c), ('rstd', 0)], [('f', c)])
                    stt(XB[:, c, tsl], fbuf[:, c, :], mA[:, g, 5, c:c + 1], XB[:, c, tsl], ALU.mult, ALU.add,
                        [('f', c), ('XB', c), ('mA', g, 5)], [('XB', c)])
            ring3[0] = False
            wslot[0] = 0
        dma('sp', yT[g].rearrange("(c p) n -> p c n", p=128), XB[:], XBK, [('yT', g)], "yout")

    nsem = R.emit(nc)
    return nc, len(R.ops), nsem


def _fm(v, nch):
    s = v.shape[:-1]
    return np.moveaxis(v.reshape(s + (nch, 128)), -1, 0)


_CACHE = {}


def make_in_maps(x_prompt, x_sample, cache_ckv, cache_kpe, state_ssm_re, state_ssm_im, c, c_ctx,
           w_mod, b_mod, g_mix_pre, g_mix_post, g_mlp_pre, g_mlp_post, w_in, g_q, w_uq,
           g_kv, w_ukv, w_attn_o, conv_w, conv_b, conv_ln_g, conv_ln_b, w_conv_o,
           ssm_a_re, ssm_a_im, ssm_log_dt, ssm_b_re, ssm_b_im, ssm_c_re, ssm_c_im, ssm_d,
           w_glu, b_glu, w_ssm_o, w_out, w_ff1, w_ff2):
    f32 = np.float32
    A = lambda a: np.ascontiguousarray(np.asarray(a, dtype=f32))
    L = L4

    w_in = A(w_in)
    perm = np.concatenate([np.arange(16, 32), np.arange(0, 16), np.arange(48, 64), np.arange(32, 48)])
    w_ksw = A(w_in[:, :, C_KPE:C_KPE + 64][:, :, perm])
    w_uq = A(w_uq).reshape(L, 512, NH, 192)
    w_ukv = A(w_ukv).reshape(L, 512, NH, 256)
    w_head = np.empty((L, NH, 512, 512), f32)
    for h in range(NH):
        w_head[:, h, :, 0:128] = w_uq[:, :, h, 0:128]
        w_head[:, h, :, 128:192] = w_uq[:, :, h, 128:192]
        w_head[:, h, :, 192:256] = w_uq[:, :, h, 128:192][:, :, perm]
        w_head[:, h, :, 256:384] = w_ukv[:, :, h, 0:128]
        w_head[:, h, :, 384:512] = w_ukv[:, :, h, 128:256]
    vecs = np.zeros((128, L, NV), f32)
    for col, v, nch in ((V_GPRE, g_mix_pre, 16), (V_GPOST, g_mix_post, 16), (V_GMPRE, g_mlp_pre, 16),
                        (V_GMPOST, g_mlp_post, 16), (V_BMOD, b_mod, 96), (V_GQ, g_q, 4), (V_GKV, g_kv, 4),
                        (V_CB, conv_b, 8), (V_LNG, conv_ln_g, 8), (V_LNB, conv_ln_b, 8), (V_SD, ssm_d, 8),
                        (V_BGLU, b_glu, 8)):
        vecs[:, :, col:col + nch] = _fm(A(v), nch)
    cw = A(conv_w)
    vecs[:, :, V_CW:V_CW + 248] = np.transpose(cw.reshape(L, 31, 8, 128), (3, 0, 2, 1)).reshape(128, L, 248)
    def sm(a):
        return np.transpose(a.reshape(L, 2, 32, 2, 64), (0, 3, 4, 1, 2)).reshape(L, 128, 64)
    ldt = np.broadcast_to(A(ssm_log_dt)[:, :, :, None], (L, 2, 64, 64))
    ssm_s = A(np.stack([sm(A(ssm_a_re)), sm(A(ssm_a_im)), sm(A(ldt))], axis=1))
    bre = A(ssm_b_re).reshape(L, 2, 8, 8, 64, 16)
    bim = A(ssm_b_im).reshape(L, 2, 8, 8, 64, 16)
    b_x = np.zeros((L, 2, 8, 2, 128, 512), f32)
    for gl in range(8):
        b_x[:, :, :, 0, gl * 16:(gl + 1) * 16, gl * 64:(gl + 1) * 64] = np.swapaxes(bre[:, :, :, gl], -1, -2)
        b_x[:, :, :, 1, gl * 16:(gl + 1) * 16, gl * 64:(gl + 1) * 64] = np.swapaxes(bim[:, :, :, gl], -1, -2)
    cre = A(ssm_c_re).reshape(L, 2, 32, 2, 16, 64)
    cim = A(ssm_c_im).reshape(L, 2, 32, 2, 16, 64)
    c_x = np.zeros((L, 2, 32, 2, 128, 128), f32)
    for t in range(32):
        for g2 in range(2):
            gl = (2 * t + g2) % 8
            c_x[:, :, t, 0, g2 * 64:(g2 + 1) * 64, gl * 16:(gl + 1) * 16] = np.swapaxes(cre[:, :, t, g2], -1, -2)
            c_x[:, :, t, 1, g2 * 64:(g2 + 1) * 64, gl * 16:(gl + 1) * 16] = np.swapaxes(cim[:, :, t, g2], -1, -2)
    pos = np.arange(NT)
    inv = (10000.0 ** (-np.arange(16, dtype=np.float32) / 16)).astype(f32)
    ang = np.stack([(pos // 64).astype(f32)[:, None] * inv, (pos % 64).astype(f32)[:, None] * inv], axis=1)
    cos, sin = np.cos(ang).astype(f32), np.sin(ang).astype(f32)
    rope = np.zeros((2, 64, NT), f32)
    for ax in range(2):
        for hf in range(2):
            rows = slice(ax * 32 + hf * 16, ax * 32 + hf * 16 + 16)
            rope[0, rows, :] = cos[:, ax, :].T
            rope[1, rows, :] = (-sin[:, ax, :].T) if hf == 0 else sin[:, ax, :].T
    shared = dict(w_mod=A(w_mod), w_in=w_in, w_ksw=w_ksw, w_head=w_head, w_ao=A(w_attn_o), w_co=A(w_conv_o),
                  w_glu=A(w_glu), w_so=A(w_ssm_o), w_out=A(w_out), w_ff1=A(w_ff1), w_ff2=A(w_ff2), vecs=vecs,
                  ssm_s=ssm_s, b_x=b_x, c_x=c_x, rope=rope, ident=np.eye(128, dtype=f32))
    xp, xsm = A(x_prompt), A(x_sample)
    cc, cctx = A(c), A(c_ctx)
    ckv, kpe = A(cache_ckv), A(cache_kpe)
    sre, sim_ = A(state_ssm_re), A(state_ssm_im)
    in_maps = []
    for core in range(8):
        xTc = np.empty((2, D, NT), f32)
        xTc[0] = xp[4 * core:4 * core + 4].reshape(NT, D).T
        xTc[1] = xsm[core].T
        cTc = np.stack([_fm(cctx, 16), _fm(cc[core], 16)], axis=-1)
        m = dict(shared)
        m.update(xT=xTc, cT=A(cTc), ckvc=A(np.swapaxes(ckv[core], -1, -2)), kpec=A(np.swapaxes(kpe[core], -1, -2)),
                 h0=A(np.stack([sm(sre[core]), sm(sim_[core])], axis=1)))
        in_maps.append(m)
    return in_maps


def assemble(results):
    f32 = np.float32
    L = L4
    y_prompt = np.empty((32, 256, D), f32)
    y_sample = np.empty((8, 1024, D), f32)
    new_ckv = np.empty((32, L, 256, 512), f32)
    new_kpe = np.empty((32, L, 256, 64), f32)
    new_re = np.empty((32, L, 2, 64, 64), f32)
    new_im = np.empty((32, L, 2, 64, 64), f32)
    for core in range(8):
        r = results[core]
        y_prompt[4 * core:4 * core + 4] = r["yT"][0].T.reshape(4, 256, D)
        y_sample[core] = r["yT"][1].T
        new_ckv[4 * core:4 * core + 4] = np.transpose(r["ckvT"].reshape(L, 512, 4, 256), (2, 0, 3, 1))
        new_kpe[4 * core:4 * core + 4] = np.transpose(r["kpeT"].reshape(L, 64, 4, 256), (2, 0, 3, 1))
        fin = r["fin"].reshape(L, 2, 2, 64, 4, 2, 32)
        fin = np.transpose(fin, (1, 4, 0, 5, 6, 2, 3)).reshape(2, 4, L, 2, 64, 64)
        new_re[4 * core:4 * core + 4] = fin[0]
        new_im[4 * core:4 * core + 4] = fin[1]
    return (y_prompt, y_sample, new_ckv, new_kpe, new_re, new_im)


def kernel(**inputs):
    if 'nc' not in _CACHE:
        _CACHE['nc'] = build_program()
    nc, nops, nsem = _CACHE['nc']
    in_maps = make_in_maps(**inputs)
    res = run_bass_kernel_spmd(nc, in_maps, core_ids=list(range(8)))
    return assemble(res.results)
```

```python
import math
import numpy as np
import concourse.bass as bass
import concourse.mybir as mybir
from concourse.bass_utils import run_bass_kernel_spmd

F32 = mybir.dt.float32
BF16 = mybir.dt.bfloat16
AF = mybir.ActivationFunctionType
ALU = mybir.AluOpType

D = 2048
L4 = 4
NT = 1024
NH = 16
EPS = 1e-6
IN_COLS = 10304
C_CQ, C_CKV, C_KPE, C_CVA, C_CVB, C_SSM, C_GA, C_GC, C_GS = 0, 512, 1024, 1088, 2112, 3136, 4160, 6208, 8256
PI = math.pi

V_GPRE, V_GPOST, V_GMPRE, V_GMPOST, V_BMOD, V_GQ, V_GKV, V_CB, V_LNG, V_LNB, V_SD, V_BGLU, V_CW = (
    0, 16, 32, 48, 64, 160, 164, 168, 176, 184, 192, 200, 208)
NV = 208 + 8 * 31


class Rec:
    def __init__(self):
        self.ops = []

    def op(self, eng, fn, r=(), w=(), chan=None):
        self.ops.append((eng, fn, tuple(r), tuple(w), chan))

    def barrier(self):
        self.ops.append((None, None, (), (), None))

    def emit(self, nc):
        ops = self.ops
        n = len(ops)
        engs = {'pe': nc.tensor, 'dve': nc.vector, 'act': nc.scalar, 'pool': nc.gpsimd, 'sp': nc.sync}
        last_w, readers = {}, {}
        deps = [None] * n
        last_on, pending = {}, {}
        for i, (eng, fn, r, w, chan) in enumerate(ops):
            if eng is None:
                snapb = set(last_on.values())
                for e_ in engs:
                    pending[e_] = pending.get(e_, set()) | snapb
                deps[i] = set()
                continue
            d = set()
            if eng in pending:
                d |= pending.pop(eng)
            for k in r:
                j = last_w.get(k)
                if j is not None:
                    d.add(j)
            for k in w:
                j = last_w.get(k)
                if j is not None:
                    d.add(j)
                rl = readers.get(k)
                if rl:
                    d.update(rl)
            d.discard(i)
            deps[i] = d
            for k in r:
                readers.setdefault(k, []).append(i)
            for k in w:
                last_w[k] = i
                readers[k] = []
            last_on[('c', chan) if chan is not None else eng] = i
        prod = [(('c', o[4]) if o[4] is not None else o[0]) for o in ops]
        clock = {e: {} for e in engs}
        snap = [None] * n
        waits = [None] * n
        signal = [False] * n
        for i, (eng, fn, r, w, chan) in enumerate(ops):
            if eng is None:
                waits[i] = []
                snap[i] = {}
                continue
            ck = clock[eng]
            need = []
            for j in sorted(deps[i], reverse=True):
                p = prod[j]
                if p == 'pe' and eng == 'pe' and chan is None:
                    continue
                if ck.get(p, -1) >= j:
                    continue
                need.append(j)
                ck[p] = j
                sj = snap[j]
                for kk, vv in sj.items():
                    if ck.get(kk, -1) < vv:
                        ck[kk] = vv
            for j in need:
                signal[j] = True
            waits[i] = need
            snap[i] = dict(ck)
        cnt = {}
        val = [0] * n
        for i, (eng, fn, r, w, chan) in enumerate(ops):
            if eng is None:
                continue
            if chan is not None:
                cnt[('c', chan)] = cnt.get(('c', chan), 0) + 16
                val[i] = cnt[('c', chan)]
            elif signal[i]:
                cnt[eng] = cnt.get(eng, 0) + 1
                val[i] = cnt[eng]
        sems = {}

        def sem_of(p):
            if p not in sems:
                nm = p if isinstance(p, str) else "c_" + str(p[1])
                sems[p] = nc.alloc_semaphore("s_" + nm)
            return sems[p]

        for i, (eng, fn, r, w, chan) in enumerate(ops):
            if eng is None:
                continue
            e = engs[eng]
            for j in waits[i]:
                e.wait_ge(sem_of(prod[j]), val[j])
            ins = fn(e)
            if chan is not None:
                ins.then_inc(sem_of(('c', chan)), 16)
            elif signal[i]:
                ins.then_inc(sem_of(eng), 1)
        for p, c in cnt.items():
            nc.sync.wait_ge(sem_of(p), c)
        return len(sems)


class Arena:
    _cnt = [0]

    def __init__(self, nc, base, size):
        self.nc, self.base, self.size, self.off, self.n = nc, base, size, 0, 0

    def reset(self, off=0):
        self.off = off

    def alloc(self, shape, dt):
        nb = int(np.prod(shape[1:])) * (2 if dt == BF16 else 4)
        nb = (nb + 31) // 32 * 32
        assert self.off + nb <= self.size, f"arena overflow {self.off}+{nb}>{self.size} for {shape}"
        Arena._cnt[0] += 1
        t = self.nc.alloc_sbuf_tensor_at(f"ar{Arena._cnt[0]}", list(shape), dt, offset=self.base + self.off)
        self.off += nb
        return t


def build_program(n_layers=L4, groups=(0, 1), debug=False):
    nc = bass.Bass("TRN2", target_bir_lowering=False)
    R = Rec()

    def din(name, shape):
        return nc.dram_tensor(name, list(shape), F32, kind="ExternalInput").ap()

    def dout(name, shape):
        return nc.dram_tensor(name, list(shape), F32, kind="ExternalOutput").ap()

    xT = din("xT", [2, D, NT])
    cT = din("cT", [128, 16, 2])
    ckvc = din("ckvc", [L4, 512, 512])
    kpec = din("kpec", [L4, 64, 512])
    h0d = din("h0", [L4, 2, 128, 64])
    w_mod = din("w_mod", [L4, D, 6 * D])
    w_in = din("w_in", [L4, D, IN_COLS])
    w_ksw = din("w_ksw", [L4, D, 64])
    w_head = din("w_head", [L4, NH, 512, 512])
    w_ao = din("w_ao", [L4, D, D])
    w_co = din("w_co", [L4, 1024, D])
    w_glu = din("w_glu", [L4, 1024, 1024])
    w_so = din("w_so", [L4, 1024, D])
    w_out = din("w_out", [L4, D, D])
    w_ff1 = din("w_ff1", [L4, D, 4 * D])
    w_ff2 = din("w_ff2", [L4, 4 * D, D])
    vecs_d = din("vecs", [128, L4, NV])
    ssm_s = din("ssm_s", [L4, 3, 128, 64])
    b_x = din("b_x", [L4, 2, 8, 2, 128, 512])
    c_x = din("c_x", [L4, 2, 32, 2, 128, 128])
    rope_d = din("rope", [2, 64, NT])
    ident_d = din("ident", [128, 128])
    yT = dout("yT", [2, D, NT])
    ckvT = dout("ckvT", [L4, 512, NT])
    kpeT = dout("kpeT", [L4, 64, NT])
    fin_d = dout("fin", [L4, 2, 128, 256])
    xs_d = nc.dram_tensor("xs_scratch", [D, NT], F32, kind="Internal").ap()

    base = (nc.sbuf_base + 31) // 32 * 32
    top = nc.sbuf_top
    off = [base]

    def salloc(name, shape, dt):
        nb = int(np.prod(shape[1:])) * (2 if dt == BF16 else 4)
        nb = (nb + 31) // 32 * 32
        t = nc.alloc_sbuf_tensor_at(name, list(shape), dt, offset=off[0])
        off[0] += nb
        return t

    H = salloc("H", [128, 16, NT], BF16)
    NSLOT = 2
    WS = [salloc(f"WS{i}", [128, 4096], BF16) for i in range(NSLOT)]
    gs_off = off[0]
    GS = [salloc(f"GS{i}", [128, 2048], BF16) for i in range(2)]
    GSW = nc.alloc_sbuf_tensor_at("GSW", [128, 4096], BF16, offset=gs_off)
    vecs = salloc("vecs", [128, L4, NV], F32)
    mods = salloc("mods", [128, L4, 96, 2], F32)
    mA = salloc("mA", [128, 2, 6, 16], F32)
    onesb = salloc("onesb", [128, 128], BF16)
    identb = salloc("identb", [128, 128], BF16)
    identf = salloc("identf", [128, 128], F32)
    onesf = salloc("onesf", [128, 128], F32)
    epsb = salloc("epsb", [128, 1], F32)
    negpi = salloc("negpi", [128, 1], F32)
    FIN = salloc("FIN", [128, 2, 4, 2, 32], F32)
    s2 = salloc("s2", [128, 16, 2], BF16)
    cTt = salloc("cTt", [128, 16, 2], F32)
    dbgs = salloc("dbgs", [128, 8], F32)
    xb_off = off[0]
    XB = salloc("XB", [128, 16, NT], F32)
    MIX = nc.alloc_sbuf_tensor_at("MIX", [128, 16, NT], BF16, offset=xb_off)
    ARX = Arena(nc, xb_off + 32768, 32768)
    AR = Arena(nc, off[0], top - off[0])
    PSt = nc.alloc_psum_tensor("PS", [128, 8, 512], F32)

    def PS(b):
        return PSt[:, b, :]

    def PK(b):
        return ('PS', b)

    def mm(out, lhsT, rhs, start, stop, r, w):
        R.op('pe', lambda e: e.matmul(out, lhsT=lhsT, rhs=rhs, start=start, stop=stop), r, w)

    def act(out, in_, func, r, w, bias=None, scale=None):
        kw = {}
        if bias is not None:
            kw['bias'] = bias
        if scale is not None:
            kw['scale'] = scale
        R.op('act', lambda e: e.activation(out=out, in_=in_, func=func, **kw), r, w)

    def tt(out, a, b, op, r, w, eng='dve'):
        R.op(eng, lambda e: e.tensor_tensor(out=out, in0=a, in1=b, op=op), r, w)

    def ts(out, a, s1, s2_, op0, op1, r, w, eng='dve'):
        if s2_ is None:
            R.op(eng, lambda e: e.tensor_scalar(out=out, in0=a, scalar1=s1, scalar2=None, op0=op0), r, w)
        else:
            R.op(eng, lambda e: e.tensor_scalar(out=out, in0=a, scalar1=s1, scalar2=s2_, op0=op0, op1=op1), r, w)

    def stt(out, a, s, b, op0, op1, r, w, eng='dve'):
        R.op(eng, lambda e: e.scalar_tensor_tensor(out=out, in0=a, scalar=s, in1=b, op0=op0, op1=op1), r, w)

    def cp(out, in_, r, w, eng='dve'):
        if eng == 'act':
            R.op(eng, lambda e: e.activation(out=out, in_=in_, func=AF.Copy), r, w)
        else:
            R.op(eng, lambda e: e.tensor_copy(out=out, in_=in_), r, w)

    def recip(out, in_, r, w):
        R.op('dve', lambda e: e.reciprocal(out=out, in_=in_), r, w)

    def mset(ap, v, w, eng='dve'):
        R.op(eng, lambda e: e.memset(ap, v), (), w)

    def dma(q, out, in_, r, w, chan):
        R.op(q, lambda e: e.dma_start(out=out, in_=in_), r, w, chan)

    class Pool:
        def __init__(self, banks):
            self.banks, self.i = list(banks), 0

        def get(self, n=1):
            out = []
            for _ in range(n):
                out.append(self.banks[self.i % len(self.banks)])
                self.i += 1
            return out

    dbg_n = [0]

    def dbg(name, ap, keys):
        if not debug:
            return
        dbg_n[0] += 1
        o = nc.dram_tensor("dbg_" + name, list(ap.shape), F32, kind="ExternalOutput").ap()
        dk = ('dbgout', name)
        dma('pool', o, ap, list(keys), [dk], f"dbg{dbg_n[0]}")
        R.barrier()

    wslot = [0]

    ring3 = [False]

    def wload(src3, a, b, extra_r=()):
        nsl = 3 if ring3[0] else NSLOT
        s = wslot[0] % nsl
        wslot[0] += 1
        assert a * b <= 4096
        t_ = WS[s] if s < NSLOT else GSW
        view = t_[:, 0:a * b].rearrange("p (a b) -> p a b", a=a)
        dma('pool', view, src3, extra_r, [('WS', s)], f"ws{s}")
        return view, ('WS', s)

    def kview(w2d):
        return w2d.rearrange("(kc p) n -> p kc n", p=128)

    mset(onesb[:], 1.0, [('onesb',)])
    mset(onesf[:], 1.0, [('onesf',)])
    mset(epsb[:], EPS, [('epsb',)])
    mset(negpi[:], -PI, [('negpi',)])
    dma('sp', identf[:], ident_d[:, :], (), [('identf',)], "m_ident")
    cp(identb[:], identf[:], [('identf',)], [('identb',)])
    dma('sp', vecs[:], vecs_d[:, :, :], (), [('vecs',)], "m_vecs")
    dma('sp', cTt[:], cT[:, :, :], (), [('cTt',)], "m_ct")
    act(s2[:], cTt[:], AF.Silu, [('cTt',)], [('s2',)])

    gslot = [0]

    def gload(src3):
        s = gslot[0] % 2
        gslot[0] += 1
        view = GS[s][:, 0:2048].rearrange("p (a b) -> p a b", a=16)
        dma('pool', view, src3, (), [('GS', s)], f"gs{s}")
        return view, ('GS', s)

    def mods_load(l, q):
        return wload(kview(w_mod[l])[:, :, q * 256:(q + 1) * 256], 16, 256)

    def mods_mm(l, q, wv, wk, bank):
        for j in range(2):
            nb = q * 2 + j
            for kc in range(16):
                mm(PS(bank)[:, 2 * nb:2 * nb + 2], wv[:, kc, j * 128:(j + 1) * 128], s2[:, kc, :],
                   kc == 0, kc == 15, [wk, ('s2',)], [PK(bank)])

    def mods_finish(l, bank):
        for pth in range(2):
            tt(mods[:, l, :, pth], PS(bank)[:, 0:192].rearrange("p (n t) -> p n t", t=2)[:, :, pth],
               vecs[:, l, V_BMOD:V_BMOD + 96], ALU.add, [PK(bank), ('vecs',)], [('mods', l)])

    hide_mods = (0 in groups)
    for l in range(n_layers if not hide_mods else 1):
        for q in range(48):
            wv, wk = mods_load(l, q)
            mods_mm(l, q, wv, wk, 0)
        mods_finish(l, 0)

    def rms_rstd(chunks, nfeat, ntok, acc_banks, sq_tiles, rstd_ap, tmp_ap, tmp_key):
        nn = ntok // 512
        for i, (a, k) in enumerate(chunks):
            sq, sqk = sq_tiles[i % len(sq_tiles)]
            act(sq[:, :ntok], a, AF.Square, list(k), [sqk])
            for n_ in range(nn):
                mm(PS(acc_banks[n_]), onesb[:, :], sq[:, n_ * 512:(n_ + 1) * 512], i == 0, i == len(chunks) - 1,
                   [('onesb',), sqk], [PK(acc_banks[n_])])
        for n_ in range(nn):
            sl = slice(n_ * 512, (n_ + 1) * 512)
            act(tmp_ap[:, sl], PS(acc_banks[n_]), AF.Sqrt, [PK(acc_banks[n_]), ('epsb',)], [tmp_key],
                bias=epsb[:, 0:1], scale=1.0 / nfeat)
            recip(rstd_ap[:, sl], tmp_ap[:, sl], [tmp_key], [('rstd', n_)])

    RK = [('rstd', 0), ('rstd', 1)]

    for g in groups:
        lat = (g == 1)
        nseq, Ls = (1, 1024) if lat else (4, 256)
        NK = 1536 if lat else 1024
        NKT = NK // 128
        XBK = [('XB', c) for c in range(16)]
        dma('sp', XB[:], xT[g].rearrange("(c p) n -> p c n", p=128), (), XBK, "xin")

        for l in range(n_layers):
            def mv(j):
                return mods[:, l, j * 16:(j + 1) * 16, g]
            for half, (vg_pre, vg_post) in enumerate(((V_GPRE, V_GPOST), (V_GMPRE, V_GMPOST))):
                jb = half * 3
                stt(mA[:, g, jb + 0, :], mv(jb + 1), 1.0, vecs[:, l, vg_pre:vg_pre + 16], ALU.add, ALU.mult,
                    [('mods', l), ('vecs',)], [('mA', g, jb)])
                cp(mA[:, g, jb + 1, :], mv(jb + 0), [('mods', l)], [('mA', g, jb + 1)])
                tt(mA[:, g, jb + 2, :], mv(jb + 2), vecs[:, l, vg_post:vg_post + 16], ALU.mult,
                   [('mods', l), ('vecs',)], [('mA', g, jb + 2)])

            R.barrier()
            AR.reset()
            rstd = AR.alloc([128, NT], F32)
            sqt = [(AR.alloc([128, NT], BF16), ('sq', i)) for i in range(2)]
            xt_ = [(AR.alloc([128, NT], F32), ('xt', i)) for i in range(2)]
            rtmp, rtk = xt_[1]
            a_mark = AR.off
            rms_rstd([(XB[:, c, :], [('XB', c)]) for c in range(16)], D, NT, [0, 1], sqt, rstd, rtmp, rtk)
            for c in range(16):
                t_, tk = xt_[c % 2]
                tt(t_[:], XB[:, c, :], rstd[:], ALU.mult, [('XB', c)] + RK, [tk])
                act(H[:, c, :], t_[:], AF.Identity, [tk, ('mA', g, 0), ('mA', g, 1)], [('H', c)],
                    bias=mA[:, g, 1, c:c + 1], scale=mA[:, g, 0, c:c + 1])
            dma('sp', xs_d.rearrange("(c p) n -> p c n", p=128), XB[:], XBK + [('ARX',)] + [('MIX', c) for c in range(16)],
                [('xs',)], "xsp")
            win_v = kview(w_in[l])

            def branch_out(wmat_l, acts, gate_col, first, sig, gtmp):
                nk = len(acts)
                per = 4096 // (nk * 128)
                pb = Pool([0, 1, 2, 3, 4, 5, 6, 7])
                wmv = kview(wmat_l)
                for oc0 in range(0, 16, per):
                    wv, wk = wload(wmv[:, :, oc0 * 128:(oc0 + per) * 128], nk, per * 128)
                    for j in range(per):
                        oc = oc0 + j
                        gv, gk = gload(win_v[:, :, gate_col + oc * 128: gate_col + (oc + 1) * 128])
                        ba = pb.get(2)
                        bg = pb.get(2)
                        for kc in range(16):
                            for n_ in range(2):
                                mm(PS(bg[n_]), gv[:, kc, :], H[:, kc, n_ * 512:(n_ + 1) * 512],
                                   kc == 0, kc == 15, [gk, ('H', kc)], [PK(bg[n_])])
                        for kc in range(nk):
                            a, ak = acts[kc]
                            for n_ in range(2):
                                mm(PS(ba[n_]), wv[:, kc, j * 128:(j + 1) * 128], a[:, n_ * 512:(n_ + 1) * 512],
                                   kc == 0, kc == nk - 1, [wk, ('ARX',)] + list(ak), [PK(ba[n_])])
                        for n_ in range(2):
                            sl = slice(n_ * 512, (n_ + 1) * 512)
                            sg, sgk = sig[n_]
                            act(sg[:], PS(bg[n_]), AF.Sigmoid, [PK(bg[n_])], [sgk])
                            if first:
                                tt(MIX[:, oc, sl], PS(ba[n_]), sg[:], ALU.mult, [PK(ba[n_]), sgk], [('MIX', oc)])
                            else:
                                gt, gtk = gtmp[n_]
                                tt(gt[:], PS(ba[n_]), sg[:], ALU.mult, [PK(ba[n_]), sgk], [gtk])
                                tt(MIX[:, oc, sl], MIX[:, oc, sl], gt[:], ALU.add, [('MIX', oc), gtk], [('MIX', oc)])

            R.barrier()
            AR.reset(a_mark)
            ARX.reset()
            oall = ARX.alloc([128, 16, NT], BF16)
            raw = nc.alloc_sbuf_tensor_at(f"raw{g}_{l}", [128, 4, NT], F32, offset=ARX.base)
            cqn = AR.alloc([128, 4, NT], BF16)
            ckvk = AR.alloc([128, 4, NK], BF16)
            kpek = AR.alloc([128, NK], BF16)
            PT = [(AR.alloc([128, 512], BF16), ('PT', i)) for i in range(3)]
            rden = AR.alloc([128, 512], F32)
            m2 = AR.off
            qn = AR.alloc([128, NT], BF16)
            qr = AR.alloc([128, NT], BF16)
            Kh = AR.alloc([128, NK], BF16)
            Vh = AR.alloc([128, NKT, 128], BF16)
            if lat:
                ropet = AR.alloc([64, 2, NT], F32)
                dma('sp', ropet[:], rope_d.rearrange("t p n -> p t n"), (), [('rope',)], "m_rope")
                rt1 = AR.alloc([64, 512], F32)
                rt2 = AR.alloc([64, 512], F32)
            pp = Pool([2, 3, 4, 5, 6, 7])
            if lat:
                dma('pool', ckvk[:, :, 1024:1536], ckvc[l].rearrange("(kc p) n -> p kc n", p=128), (),
                    [('ckvk', 'c')], "cache0")
                dma('pool', kpek[:64, 1024:1536], kpec[l], (), [('kpek', 'c')], "cache1")

            for which, (col0, vg) in enumerate(((C_CQ, V_GQ), (C_CKV, V_GKV))):
                for mt2 in range(2):
                    wv, wk = wload(win_v[:, :, col0 + mt2 * 256: col0 + (mt2 + 1) * 256], 16, 256)
                    for j in range(2):
                        mt = mt2 * 2 + j
                        bks = pp.get(2)
                        for kc in range(16):
                            for n_ in range(2):
                                mm(PS(bks[n_]), wv[:, kc, j * 128:(j + 1) * 128], H[:, kc, n_ * 512:(n_ + 1) * 512],
                                   kc == 0, kc == 15, [wk, ('H', kc)], [PK(bks[n_])])
                        for n_ in range(2):
                            cp(raw[:, mt, n_ * 512:(n_ + 1) * 512], PS(bks[n_]), [PK(bks[n_]), ('ARX',)], [('raw', mt, n_), ('ARX',)],
                               eng='act' if n_ else 'dve')
                rms_rstd([(raw[:, mt, :], [('raw', mt, 0), ('raw', mt, 1)]) for mt in range(4)], 512, NT, [0, 1], sqt, rstd, rtmp, rtk)
                for mt in range(4):
                    t_, tk = xt_[0]
                    tt(t_[:], raw[:, mt, :], rstd[:], ALU.mult, [('raw', mt, 0), ('raw', mt, 1)] + RK, [tk])
                    if which == 0:
                        act(cqn[:, mt, :], t_[:], AF.Copy, [tk, ('vecs',)], [('cqn', mt)], scale=vecs[:, l, vg + mt:vg + mt + 1])
                    else:
                        ts(t_[:], t_[:], vecs[:, l, vg + mt:vg + mt + 1], None, ALU.mult, None, [tk, ('vecs',)], [tk])
                        cp(ckvk[:, mt, 0:NT], t_[:], [tk], [('ckvk', mt)], eng='act')
                        if not lat:
                            dma('sp', ckvT[l, mt * 128:(mt + 1) * 128, :], t_[:], [tk], [('ckvT', l, mt)], "ckvo")
            wv, wk = wload(win_v[:, :, C_KPE:C_KPE + 64], 16, 64)
            bks = pp.get(2)
            for kc in range(16):
                for n_ in range(2):
                    mm(PS(bks[n_])[:64, :], wv[:, kc, :], H[:, kc, n_ * 512:(n_ + 1) * 512], kc == 0, kc == 15,
                       [wk, ('H', kc)], [PK(bks[n_])])
            if not lat:
                t_, tk = xt_[0]
                for n_ in range(2):
                    cp(t_[:64, n_ * 512:(n_ + 1) * 512], PS(bks[n_])[:64, :], [PK(bks[n_])], [tk])
                cp(kpek[:64, 0:NT], t_[:64, :], [tk], [('kpek', 'o')], eng='act')
                dma('sp', kpeT[l, :, :], t_[:64, :], [tk], [('kpeT', l)], "kpeo")
            else:
                wv2, wk2 = wload(kview(w_ksw[l]), 16, 64)
                bks2 = pp.get(2)
                for kc in range(16):
                    for n_ in range(2):
                        mm(PS(bks2[n_])[:64, :], wv2[:, kc, :], H[:, kc, n_ * 512:(n_ + 1) * 512], kc == 0, kc == 15,
                           [wk2, ('H', kc)], [PK(bks2[n_])])
                for n_ in range(2):
                    sl = slice(n_ * 512, (n_ + 1) * 512)
                    tt(rt1[:], PS(bks[n_])[:64, :], ropet[:, 0, sl], ALU.mult, [PK(bks[n_]), ('rope',)], [('rt1',)])
                    tt(rt2[:], PS(bks2[n_])[:64, :], ropet[:, 1, sl], ALU.mult, [PK(bks2[n_]), ('rope',)], [('rt2',)])
                    tt(kpek[:64, sl], rt1[:], rt2[:], ALU.add, [('rt1',), ('rt2',)], [('kpek', 'o', n_)])
            kpek_keys = [('kpek', 'o')] if not lat else [('kpek', 'o', 0), ('kpek', 'o', 1), ('kpek', 'c')]
            ckvk_keys = [('ckvk', mt) for mt in range(4)] + ([('ckvk', 'c')] if lat else [])
            sc = 1.0 / math.sqrt(192.0)

            pq = Pool([2, 3, 4, 5, 6, 7])
            oacc = Pool([0, 1])
            for hd in range(NH):
                wh, whk = wload(w_head[l, hd].rearrange("(kc p) n -> p kc n", p=128), 4, 512)
                bks = pq.get(2)
                for kc in range(4):
                    for n_ in range(2):
                        mm(PS(bks[n_]), wh[:, kc, 0:128], cqn[:, kc, n_ * 512:(n_ + 1) * 512], kc == 0, kc == 3,
                           [whk, ('cqn', kc)], [PK(bks[n_])])
                for n_ in range(2):
                    cp(qn[:, n_ * 512:(n_ + 1) * 512], PS(bks[n_]), [PK(bks[n_])], [('qn', n_)], eng='act')
                bks = pq.get(2)
                for kc in range(4):
                    for n_ in range(2):
                        mm(PS(bks[n_])[:64, :], wh[:, kc, 128:192], cqn[:, kc, n_ * 512:(n_ + 1) * 512], kc == 0, kc == 3,
                           [whk, ('cqn', kc)], [PK(bks[n_])])
                if not lat:
                    for n_ in range(2):
                        cp(qr[:64, n_ * 512:(n_ + 1) * 512], PS(bks[n_])[:64, :], [PK(bks[n_])], [('qr', n_)], eng='act')
                else:
                    bks2 = pq.get(2)
                    for kc in range(4):
                        for n_ in range(2):
                            mm(PS(bks2[n_])[:64, :], wh[:, kc, 192:256], cqn[:, kc, n_ * 512:(n_ + 1) * 512], kc == 0, kc == 3,
                               [whk, ('cqn', kc)], [PK(bks2[n_])])
                    for n_ in range(2):
                        sl = slice(n_ * 512, (n_ + 1) * 512)
                        tt(rt1[:], PS(bks[n_])[:64, :], ropet[:, 0, sl], ALU.mult, [PK(bks[n_]), ('rope',)], [('rt1',)])
                        tt(rt2[:], PS(bks2[n_])[:64, :], ropet[:, 1, sl], ALU.mult, [PK(bks2[n_]), ('rope',)], [('rt2',)])
                        tt(qr[:64, sl], rt1[:], rt2[:], ALU.add, [('rt1',), ('rt2',)], [('qr', n_)])
                nkb = NK // 512
                bks = pq.get(nkb)
                for kc in range(4):
                    for n_ in range(nkb):
                        mm(PS(bks[n_]), wh[:, kc, 256:384], ckvk[:, kc, n_ * 512:(n_ + 1) * 512], kc == 0, kc == 3,
                           [whk] + ckvk_keys, [PK(bks[n_])])
                for n_ in range(nkb):
                    cp(Kh[:, n_ * 512:(n_ + 1) * 512], PS(bks[n_]), [PK(bks[n_])], [('Kh', n_)], eng='act' if n_ % 2 else 'dve')
                for kt4 in range(NKT // 4):
                    bk = pq.get(1)[0]
                    for j in range(4):
                        kt = kt4 * 4 + j
                        for kc in range(4):
                            mm(PS(bk)[:, j * 128:(j + 1) * 128], ckvk[:, kc, kt * 128:(kt + 1) * 128], wh[:, kc, 384:512],
                               kc == 0, kc == 3, [whk] + ckvk_keys, [PK(bk)])
                    cp(Vh[:, kt4 * 4:(kt4 + 1) * 4, :], PS(bk).rearrange("p (a b) -> p a b", a=4), [PK(bk)], [('Vh', kt4)],
                       eng='act' if kt4 % 2 else 'dve')
                if lat:
                    qblocks = [(qt * 512, 512, list(range(NKT))) for qt in range(2)]
                else:
                    qblocks = [(s_ * 256, 256, [2 * s_, 2 * s_ + 1]) for s_ in range(4)]
                for (q0, QN, kts) in qblocks:
                    ob = oacc.get(1)[0]
                    db = pq.get(1)[0]
                    qkeys = [('qn', q0 // 512), ('qr', q0 // 512)]
                    for i, kt in enumerate(kts):
                        sb = pq.get(1)[0]
                        if sb == db:
                            sb = pq.get(1)[0]
                        mm(PS(sb)[:, :QN], Kh[:, kt * 128:(kt + 1) * 128], qn[:, q0:q0 + QN], True, False,
                           [('Kh', kt // 4), qkeys[0]], [PK(sb)])
                        mm(PS(sb)[:, :QN], kpek[:64, kt * 128:(kt + 1) * 128], qr[:64, q0:q0 + QN], False, True,
                           kpek_keys + [qkeys[1]], [PK(sb)])
                        pt, ptk = PT[i % 3]
                        act(pt[:, :QN], PS(sb)[:, :QN], AF.Exp, [PK(sb)], [ptk], scale=sc)
                        mm(PS(ob)[:, :QN], Vh[:, kt, :], pt[:, :QN], i == 0, i == len(kts) - 1, [('Vh', kt // 4), ptk], [PK(ob)])
                        mm(PS(db)[:, :QN], onesb[:, :], pt[:, :QN], i == 0, i == len(kts) - 1, [('onesb',), ptk], [PK(db)])
                    recip(rden[:, :QN], PS(db)[:, :QN], [PK(db)], [('rden',)])
                    tt(oall[:, hd, q0:q0 + QN], PS(ob)[:, :QN], rden[:, :QN], ALU.mult, [PK(ob), ('rden',), ('ARX',)],
                       [('oall', hd, q0 // 512), ('ARX',)])
            R.barrier()
            AR.reset(m2)
            sig = [(AR.alloc([128, 512], F32), ('sig', i)) for i in range(2)]
            gtmp = [(AR.alloc([128, 512], F32), ('gtmp', i)) for i in range(2)]
            if l == 0:
                dbg(f"oall{g}", oall[:, 0:2, :], [('ARX',)])
            branch_out(w_ao[l], [(oall[:, hd, :], [('oall', hd, 0), ('oall', hd, 1)]) for hd in range(NH)], C_GA, True, sig, gtmp)

            if l == 0:
                dbg(f"mixa{g}", MIX[:, 0:2, :], [('MIX', 0), ('MIX', 1)])
            R.barrier()
            AR.reset(a_mark)
            ARX.reset()
            ycv = ARX.alloc([128, 8, NT], F32)
            sig = [(AR.alloc([128, 512], F32), ('sig', i)) for i in range(2)]
            gtmp = [(AR.alloc([128, 512], F32), ('gtmp', i)) for i in range(2)]
            PADW = Ls + 30
            zpad = [(AR.alloc([128, nseq, PADW], BF16), ('zpad', i)) for i in range(2)]
            diag = AR.alloc([128, 31, 128], BF16)
            cact = AR.alloc([128, 8, NT], BF16)
            sgb = AR.alloc([128, NT], F32)
            mu = AR.alloc([128, NT], F32)
            for zp, zk in zpad:
                mset(zp[:], 0.0, [zk], eng='pool')
            pc = Pool([0, 1, 2, 3, 4, 5, 6, 7])
            for ct in range(8):
                wa, wak = wload(win_v[:, :, C_CVA + ct * 128:C_CVA + (ct + 1) * 128], 16, 128)
                wb, wbk = wload(win_v[:, :, C_CVB + ct * 128:C_CVB + (ct + 1) * 128], 16, 128)
                bb = pc.get(2)
                ba = pc.get(2)
                for kc in range(16):
                    for n_ in range(2):
                        mm(PS(bb[n_]), wb[:, kc, :], H[:, kc, n_ * 512:(n_ + 1) * 512], kc == 0, kc == 15, [wbk, ('H', kc)], [PK(bb[n_])])
                for kc in range(16):
                    for n_ in range(2):
                        mm(PS(ba[n_]), wa[:, kc, :], H[:, kc, n_ * 512:(n_ + 1) * 512], kc == 0, kc == 15, [wak, ('H', kc)], [PK(ba[n_])])
                zp, zk = zpad[ct % 2]
                for n_ in range(2):
                    act(sgb[:, n_ * 512:(n_ + 1) * 512], PS(bb[n_]), AF.Sigmoid, [PK(bb[n_])], [('sgb', n_)])
                    if Ls >= 512:
                        tt(zp[:, 0, 15 + n_ * 512: 15 + (n_ + 1) * 512], PS(ba[n_]), sgb[:, n_ * 512:(n_ + 1) * 512], ALU.mult,
                           [PK(ba[n_]), ('sgb', n_)], [zk])
                    else:
                        spb = 512 // Ls
                        tt(zp[:, n_ * spb:(n_ + 1) * spb, 15:15 + Ls], PS(ba[n_]).rearrange("p (s t) -> p s t", s=spb),
                           sgb[:, n_ * 512:(n_ + 1) * 512].rearrange("p (s t) -> p s t", s=spb), ALU.mult,
                           [PK(ba[n_]), ('sgb', n_)], [zk])
                for j in range(31):
                    cwj = vecs[:, l, V_CW + ct * 31 + j: V_CW + ct * 31 + j + 1]
                    if j % 2:
                        act(diag[:, j, :], identb[:], AF.Copy, [('identb',), ('vecs',)], [('diag', j)], scale=cwj)
                    else:
                        ts(diag[:, j, :], identb[:], cwj, None, ALU.mult, None, [('identb',), ('vecs',)], [('diag', j)])
                bo = pc.get(2)
                if Ls >= 512:
                    for j in range(31):
                        for n_ in range(2):
                            mm(PS(bo[n_]), diag[:, j, :], zp[:, 0, j + n_ * 512: j + (n_ + 1) * 512], j == 0, j == 30,
                               [('diag', j), zk], [PK(bo[n_])])
                else:
                    for s_ in range(nseq):
                        b_ = bo[s_ // 2]
                        for j in range(31):
                            mm(PS(b_)[:, (s_ % 2) * Ls:(s_ % 2 + 1) * Ls], diag[:, j, :], zp[:, s_, j:j + Ls], j == 0, j == 30,
                               [('diag', j), zk], [PK(b_)])
                for n_ in range(2):
                    act(ycv[:, ct, n_ * 512:(n_ + 1) * 512], PS(bo[n_]), AF.Identity, [PK(bo[n_]), ('vecs',), ('ARX',)],
                        [('ycv', ct, n_), ('ARX',)], bias=vecs[:, l, V_CB + ct:V_CB + ct + 1])
            for ct in range(8):
                sq, sqk = sqt[0]
                yb, ybk = sqt[1]
                yk = [('ycv', ct, 0), ('ycv', ct, 1), ('ARX',)]
                act(sq[:], ycv[:, ct, :], AF.Square, yk, [sqk])
                cp(yb[:], ycv[:, ct, :], yk, [ybk])
                for n_ in range(2):
                    mm(PS(n_), onesb[:, :], yb[:, n_ * 512:(n_ + 1) * 512], ct == 0, ct == 7, [('onesb',), ybk], [PK(n_)])
                    mm(PS(2 + n_), onesb[:, :], sq[:, n_ * 512:(n_ + 1) * 512], ct == 0, ct == 7, [('onesb',), sqk], [PK(2 + n_)])
            for n_ in range(2):
                sl = slice(n_ * 512, (n_ + 1) * 512)
                act(mu[:, sl], PS(n_), AF.Copy, [PK(n_)], [('mu', n_)], scale=1.0 / 1024)
                tt(rtmp[:, sl], mu[:, sl], mu[:, sl], ALU.mult, [('mu', n_)], [rtk])
                stt(rtmp[:, sl], PS(2 + n_), 1.0 / 1024, rtmp[:, sl], ALU.mult, ALU.subtract, [PK(2 + n_), rtk], [rtk])
                act(rtmp[:, sl], rtmp[:, sl], AF.Sqrt, [rtk, ('epsb',)], [rtk], bias=epsb[:, 0:1])
                recip(rstd[:, sl], rtmp[:, sl], [rtk], [('rstd', n_)])
            for ct in range(8):
                t_, tk = xt_[0]
                yk = [('ycv', ct, 0), ('ycv', ct, 1), ('ARX',)]
                tt(t_[:], ycv[:, ct, :], mu[:], ALU.subtract, yk + [('mu', 0), ('mu', 1)], [tk])
                tt(t_[:], t_[:], rstd[:], ALU.mult, [tk] + RK, [tk])
                act(cact[:, ct, :], t_[:], AF.Silu, [tk, ('vecs',)], [('cact', ct)],
                    bias=vecs[:, l, V_LNB + ct:V_LNB + ct + 1], scale=vecs[:, l, V_LNG + ct:V_LNG + ct + 1])
            if l == 0:
                dbg(f"cact{g}", cact[:, 0:2, :], [('cact', 0), ('cact', 1)])
            branch_out(w_co[l], [(cact[:, ct, :], [('cact', ct)]) for ct in range(8)], C_GC, False, sig, gtmp)

            if l == 0:
                dbg(f"mixc{g}", MIX[:, 0:2, :], [('MIX', 0), ('MIX', 1)])
            R.barrier()
            AR.reset(0)
            ARX.reset()
            ub = ARX.alloc([128, 8, NT], BF16)
            gb = ARX.alloc([128, 8, NT], BF16)
            prm = AR.alloc([128, 24, 64], F32)
            LPr = AR.alloc([128, 9, 64], F32)
            LPi = AR.alloc([128, 9, 64], F32)
            LPn = AR.alloc([128, 9, 64], F32)
            Qr = AR.alloc([128, 7, 64], F32)
            Qi = AR.alloc([128, 7, 64], F32)
            Qn = AR.alloc([128, 7, 64], F32)
            c0 = AR.alloc([128, 2, 64], F32)
            Bex = [[(AR.alloc([128, 512], BF16), ('Bex', d_, ri)) for ri in range(2)] for d_ in range(2)]
            bxt = [(AR.alloc([128, 512], F32), ('bxt', i)) for i in range(2)]
            btm = [(AR.alloc([128, 512], F32), ('btm', i)) for i in range(2)]
            dgf = [(AR.alloc([128, 128], F32), ('dgf', i)) for i in range(2)]
            dgD = AR.alloc([128, 128], BF16)
            NB = Ls // 8
            PADE = 64 if lat else 16
            EW = PADE + NB + PADE
            NSET = 2
            Ebuf = [[AR.alloc([128, 2, nseq, EW], F32) for pp_ in range(2)] for s_ in range(NSET)]
            Cex = [[(AR.alloc([128, 128], BF16), ('Cex', s_, ri)) for ri in range(2)] for s_ in range(2)]
            xbb = [(AR.alloc([128, 2, NT], BF16), ('xb', s_)) for s_ in range(NSET)]
            m3 = AR.off
            xri = [(AR.alloc([128, 2, NT], F32), ('x', s_)) for s_ in range(NSET)]
            for s_ in range(NSET):
                for pp_ in range(2):
                    mset(Ebuf[s_][pp_][:], 0.0, [('E', s_, pp_)], eng='pool')

            dma('sp', prm[:, 0:3, :], ssm_s[l].rearrange("t p n -> p t n"), (), [('prm',)], "m_prm")
            P_ = lambda i: prm[:, i, :]
            pk = [('prm',)]
            are, aim, ldt = P_(0), P_(1), P_(2)
            fact = [1.0 / math.factorial(k) for k in range(12)]
            ts(P_(3), ldt, 0.125, None, ALU.mult, None, pk, pk)
            ts(P_(4), P_(3), fact[10], None, ALU.mult, None, pk, pk)
            for k in range(9, 0, -1):
                stt(P_(4), P_(4), fact[k], P_(3), ALU.add, ALU.mult, pk, pk)
            ts(P_(4), P_(4), 1.0, None, ALU.add, None, pk, pk)
            for _ in range(3):
                tt(P_(4), P_(4), P_(4), ALU.mult, pk, pk)
            tt(P_(5), are, P_(4), ALU.mult, pk, pk)
            tt(P_(6), aim, P_(4), ALU.mult, pk, pk)
            ts(P_(7), P_(5), fact[6], None, ALU.mult, None, pk, pk)
            for k in range(5, 0, -1):
                stt(P_(7), P_(7), fact[k], P_(5), ALU.add, ALU.mult, pk, pk)
            ts(P_(7), P_(7), 1.0, None, ALU.add, None, pk, pk)
            act(P_(8), P_(6), AF.Sin, pk, pk, scale=0.125)
            act(P_(9), P_(6), AF.Sin, pk, pk, scale=0.0625)
            tt(P_(9), P_(9), P_(9), ALU.mult, pk, pk)
            ts(P_(9), P_(9), -2.0, 1.0, ALU.mult, ALU.add, pk, pk)
            for _ in range(3):
                tt(P_(16), P_(8), P_(9), ALU.mult, pk, pk)
                tt(P_(17), P_(9), P_(9), ALU.mult, pk, pk)
                tt(P_(18), P_(8), P_(8), ALU.mult, pk, pk)
                tt(P_(9), P_(17), P_(18), ALU.subtract, pk, pk)
                ts(P_(8), P_(16), 2.0, None, ALU.mult, None, pk, pk)
            tt(LPr[:, 1, :], P_(7), P_(9), ALU.mult, pk, [('LP',)])
            tt(LPi[:, 1, :], P_(7), P_(8), ALU.mult, pk, [('LP',)])
            tt(P_(10), are, are, ALU.mult, pk, pk)
            tt(P_(11), aim, aim, ALU.mult, pk, pk)
            tt(P_(10), P_(10), P_(11), ALU.add, pk, pk)
            recip(P_(10), P_(10), pk, pk)
            ts(P_(11), LPr[:, 1, :], -1.0, None, ALU.add, None, [('LP',)], pk)
            tt(P_(12), P_(11), are, ALU.mult, pk, pk)
            tt(P_(13), LPi[:, 1, :], aim, ALU.mult, pk + [('LP',)], pk)
            tt(P_(12), P_(12), P_(13), ALU.add, pk, pk)
            tt(P_(14), P_(12), P_(10), ALU.mult, pk, pk)
            tt(P_(12), LPi[:, 1, :], are, ALU.mult, pk + [('LP',)], pk)
            tt(P_(13), P_(11), aim, ALU.mult, pk, pk)
            tt(P_(12), P_(12), P_(13), ALU.subtract, pk, pk)
            tt(P_(15), P_(12), P_(10), ALU.mult, pk, pk)
            fre, fim = P_(14), P_(15)
            for k in range(2, 9):
                tt(P_(16), LPr[:, k - 1, :], LPr[:, 1, :], ALU.mult, [('LP',)], pk)
                tt(P_(17), LPi[:, k - 1, :], LPi[:, 1, :], ALU.mult, [('LP',)], pk)
                tt(LPr[:, k, :], P_(16), P_(17), ALU.subtract, pk, [('LP',)])
                tt(P_(16), LPr[:, k - 1, :], LPi[:, 1, :], ALU.mult, [('LP',)], pk)
                tt(P_(17), LPi[:, k - 1, :], LPr[:, 1, :], ALU.mult, [('LP',)], pk)
                tt(LPi[:, k, :], P_(16), P_(17), ALU.add, pk, [('LP',)])
            ts(LPn[:, 1:9, :], LPi[:, 1:9, :], -1.0, None, ALU.mult, None, [('LP',)], [('LPn',)])
            cp(Qr[:, 0, :], LPr[:, 8, :], [('LP',)], [('Q',)])
            cp(Qi[:, 0, :], LPi[:, 8, :], [('LP',)], [('Q',)])
            for k in range(1, 7):
                tt(P_(16), Qr[:, k - 1, :], Qr[:, k - 1, :], ALU.mult, [('Q',)], pk)
                tt(P_(17), Qi[:, k - 1, :], Qi[:, k - 1, :], ALU.mult, [('Q',)], pk)
                tt(Qr[:, k, :], P_(16), P_(17), ALU.subtract, pk, [('Q',)])
                tt(P_(16), Qr[:, k - 1, :], Qi[:, k - 1, :], ALU.mult, [('Q',)], pk)
                ts(Qi[:, k, :], P_(16), 2.0, None, ALU.mult, None, pk, [('Q',)])
            ts(Qn[:], Qi[:], -1.0, None, ALU.mult, None, [('Q',)], [('Qn',)])
            if lat:
                dma('sp', prm[:, 22:24, :], h0d[l].rearrange("t p n -> p t n"), (), [('h0',)], "m_h0")
                hk = [('h0',), ('LP',)]
                tt(P_(20), LPr[:, 1, :], P_(22), ALU.mult, hk, [('h0t',)])
                tt(P_(21), LPi[:, 1, :], P_(23), ALU.mult, hk, [('h0t',)])
                tt(c0[:, 0, :], P_(20), P_(21), ALU.subtract, [('h0t',)], [('c0',)])
                tt(P_(20), LPr[:, 1, :], P_(23), ALU.mult, hk, [('h0t',)])
                tt(P_(21), LPi[:, 1, :], P_(22), ALU.mult, hk, [('h0t',)])
                tt(c0[:, 1, :], P_(20), P_(21), ALU.add, [('h0t',)], [('c0',)])

            if l == 0:
                dbg(f"prm{g}", prm[:], pk + [('h0',), ('h0t',)] if lat else pk)
                dbg(f"LPr{g}", LPr[:, 1:9, :], [('LP',)])
                dbg(f"LPi{g}", LPi[:, 1:9, :], [('LP',)])
                dbg(f"Qr{g}", Qr[:], [('Q',)])
            pu = Pool([4, 5, 6, 7])
            for ct in range(8):
                wv, wk = wload(win_v[:, :, C_SSM + ct * 128:C_SSM + (ct + 1) * 128], 16, 128)
                bks = pu.get(2)
                for kc in range(16):
                    for n_ in range(2):
                        mm(PS(bks[n_]), wv[:, kc, :], H[:, kc, n_ * 512:(n_ + 1) * 512], kc == 0, kc == 15, [wk, ('H', kc)], [PK(bks[n_])])
                for n_ in range(2):
                    cp(ub[:, ct, n_ * 512:(n_ + 1) * 512], PS(bks[n_]), [PK(bks[n_]), ('ARX',)], [('ub', ct, n_), ('ARX',)], eng='act')

            def scan_gen(X, xk, st_, d_, col):
                eng = 'dve'
                X5 = X[:].rearrange("p r (s a b) -> p r s a b", s=nseq, b=8)
                lk = [('LP',), ('LPn',)]
                Eb = Ebuf[st_]
                ek = lambda p_: [('E', st_, p_)]

                def cstep(dst, srcv, acc, mr, mi, mn, rk, wk_):
                    stt(dst, srcv, mr, acc, ALU.mult, ALU.add, rk, wk_, eng=eng)
                    yield
                    stt(dst[:, 0], srcv[:, 1], mn, dst[:, 0], ALU.mult, ALU.add, rk, wk_, eng=eng)
                    yield
                    stt(dst[:, 1], srcv[:, 0], mi, dst[:, 1], ALU.mult, ALU.add, rk, wk_, eng=eng)
                    yield

                lam = lambda k: (LPr[:, k, col:col + 1], LPi[:, k, col:col + 1], LPn[:, k, col:col + 1])
                m1 = lam(1)
                rng = range(1, 8) if d_ == 0 else range(6, -1, -1)
                for b in rng:
                    pb_ = b - 1 if d_ == 0 else b + 1
                    yield from cstep(X5[:, :, :, :, b], X5[:, :, :, :, pb_], X5[:, :, :, :, b], m1[0], m1[1], m1[2], [xk] + lk, [xk])
                eb = 7 if d_ == 0 else 0
                cp(Eb[0][:, :, :, PADE:PADE + NB], X5[:, :, :, :, eb], [xk], ek(0), eng=eng)
                yield
                cur, k, s_ = 0, 0, 1
                while s_ < NB:
                    A_, B_ = Eb[cur], Eb[1 - cur]
                    sh = -s_ if d_ == 0 else s_
                    yield from cstep(B_[:, :, :, PADE:PADE + NB], A_[:, :, :, PADE + sh:PADE + sh + NB], A_[:, :, :, PADE:PADE + NB],
                                     Qr[:, k, col:col + 1], Qi[:, k, col:col + 1], Qn[:, k, col:col + 1],
                                     ek(cur) + [('Q',), ('Qn',)], ek(1 - cur))
                    cur = 1 - cur
                    s_ *= 2
                    k += 1
                Ef = Eb[cur]
                if d_ == 0:
                    for b in range(0, 7):
                        m = lam(b + 1)
                        yield from cstep(X5[:, :, :, :, b], Ef[:, :, :, PADE - 1:PADE - 1 + NB], X5[:, :, :, :, b], m[0], m[1], m[2],
                                         [xk] + lk + ek(cur), [xk])
                else:
                    for b in range(1, 8):
                        m = lam(8 - b)
                        yield from cstep(X5[:, :, :, :, b], Ef[:, :, :, PADE + 1:PADE + 1 + NB], X5[:, :, :, :, b], m[0], m[1], m[2],
                                         [xk] + lk + ek(cur), [xk])
                cp(X5[:, :, :, :, eb], Ef[:, :, :, PADE:PADE + NB], ek(cur), [xk], eng=eng)
                yield

            ybank = [0, 1]
            hide_now = hide_mods and g == 0 and l + 1 < n_layers
            pbu = Pool([2, 3, 4, 5, 6] if hide_now else [2, 3, 4, 5, 6, 7])
            mq = [0]
            mpend = []

            def mods_step():
                if not hide_now:
                    return
                for (q_, wv_, wk_) in mpend:
                    mods_mm(l + 1, q_, wv_, wk_, 7)
                del mpend[:]
                for _ in range(2):
                    if mq[0] < 48:
                        wv_, wk_ = mods_load(l + 1, mq[0])
                        mpend.append((mq[0], wv_, wk_))
                        mq[0] += 1
            for ct in range(8):
                ubk = [('ub', ct, 0), ('ub', ct, 1), ('ARX',)]
                for d_ in range(2):
                    fb = pbu.get(2)
                    for ri, fsrc in enumerate((fre, fim)):
                        for tl in range(4):
                            col = d_ * 32 + ct * 4 + tl
                            dg, dgk = dgf[(ri * 4 + tl) % 2]
                            ts(dg[:], identf[:], fsrc[:, col:col + 1], None, ALU.mult, None, [('identf',)] + pk, [dgk])
                            mm(PS(fb[ri])[:, tl * 128:(tl + 1) * 128], onesf[:, :], dg[:], True, True, [('onesf',), dgk], [PK(fb[ri])])
                    bre_t, brk = bxt[0]
                    bim_t, bik = bxt[1]
                    dma('sp', bre_t[:], b_x[l, d_, ct, 0], (), [brk], "bx0")
                    dma('sp', bim_t[:], b_x[l, d_, ct, 1], (), [bik], "bx1")
                    t1, t1k = btm[0]
                    t2, t2k = btm[1]
                    tt(t1[:], PS(fb[0]), bre_t[:], ALU.mult, [PK(fb[0]), brk], [t1k])
                    tt(t2[:], PS(fb[1]), bim_t[:], ALU.mult, [PK(fb[1]), bik], [t2k])
                    tt(Bex[d_][0][0][:], t1[:], t2[:], ALU.subtract, [t1k, t2k], [Bex[d_][0][1]])
                    tt(t1[:], PS(fb[0]), bim_t[:], ALU.mult, [PK(fb[0]), bik], [t1k])
                    tt(t2[:], PS(fb[1]), bre_t[:], ALU.mult, [PK(fb[1]), brk], [t2k])
                    tt(Bex[d_][1][0][:], t1[:], t2[:], ALU.add, [t1k, t2k], [Bex[d_][1][1]])
                ts(dgD[:], identb[:], vecs[:, l, V_SD + ct:V_SD + ct + 1], None, ALU.mult, None, [('identb',), ('vecs',)], [('dgD',)])
                for n_ in range(2):
                    mm(PS(ybank[n_]), dgD[:], ub[:, ct, n_ * 512:(n_ + 1) * 512], True, False, [('dgD',)] + ubk, [PK(ybank[n_])])
                tiles8 = [(d_, tl) for d_ in range(2) for tl in range(4)]
                for pr in range(4):
                    pair = tiles8[2 * pr:2 * pr + 2]
                    gens = []
                    for st_, (d_, tl) in enumerate(pair):
                        tile_ = ct * 4 + tl
                        col = d_ * 32 + tile_
                        X, xk = xri[st_]
                        (cre, crk), (cim, cik) = Cex[st_]
                        dma('pool', cre[:], c_x[l, d_, tile_, 0], (), [crk], f"cx{st_}0")
                        dma('pool', cim[:], c_x[l, d_, tile_, 1], (), [cik], f"cx{st_}1")
                        br_ = pbu.get(2)
                        bi_ = pbu.get(2) if st_ == 0 else pbu.get(2)
                        for n_ in range(2):
                            mm(PS(br_[n_]), Bex[d_][0][0][:, tl * 128:(tl + 1) * 128], ub[:, ct, n_ * 512:(n_ + 1) * 512], True, True,
                               [Bex[d_][0][1]] + ubk, [PK(br_[n_])])
                            mm(PS(bi_[n_]), Bex[d_][1][0][:, tl * 128:(tl + 1) * 128], ub[:, ct, n_ * 512:(n_ + 1) * 512], True, True,
                               [Bex[d_][1][1]] + ubk, [PK(bi_[n_])])
                        for n_ in range(2):
                            cp(X[:, 0, n_ * 512:(n_ + 1) * 512], PS(br_[n_]), [PK(br_[n_])], [xk], eng='act')
                            cp(X[:, 1, n_ * 512:(n_ + 1) * 512], PS(bi_[n_]), [PK(bi_[n_])], [xk], eng='act')
                        if lat:
                            pos = 0 if d_ == 0 else NT - 1
                            tt(X[:, :, pos], X[:, :, pos], c0[:, :, col], ALU.add, [xk, ('c0',)], [xk])
                        gens.append(scan_gen(X, xk, st_, d_, col))
                    while gens:
                        for g_ in list(gens):
                            try:
                                next(g_)
                            except StopIteration:
                                gens.remove(g_)
                    for st_, (d_, tl) in enumerate(pair):
                        tile_ = ct * 4 + tl
                        X, xk = xri[st_]
                        (cre, crk), (cim, cik) = Cex[st_]
                        if not lat:
                            fpos = Ls - 1 if d_ == 0 else 0
                            Xs = X[:].rearrange("p r (s t) -> p r s t", s=nseq)
                            cp(FIN[:, :, :, d_, tile_], Xs[:, :, :, fpos], [xk], [('FIN',)])
                        xb_, xbk = xbb[st_]
                        act(xb_[:, 0, :], X[:, 0, :], AF.Copy, [xk], [xbk])
                        act(xb_[:, 1, :], X[:, 1, :], AF.Copy, [xk], [xbk], scale=-1.0)
                        last = (pr == 3 and st_ == 1)
                        for n_ in range(2):
                            mm(PS(ybank[n_]), cre[:], xb_[:, 0, n_ * 512:(n_ + 1) * 512], False, False, [crk, xbk], [PK(ybank[n_])])
                            mm(PS(ybank[n_]), cim[:], xb_[:, 1, n_ * 512:(n_ + 1) * 512], False, last, [cik, xbk], [PK(ybank[n_])])
                    mods_step()
                for n_ in range(2):
                    act(gb[:, ct, n_ * 512:(n_ + 1) * 512], PS(ybank[n_]), AF.Gelu_apprx_tanh, [PK(ybank[n_]), ('ARX',)],
                        [('gb', ct, n_), ('ARX',)])
            if l == 0:
                dbg(f"ub{g}", ub[:, 0, :], [('ub', 0, 0), ('ub', 0, 1), ('ARX',)])
                dbg(f"Bex{g}", Bex[1][0][0][:], [Bex[1][0][1]])
                dbg(f"xr{g}", xri[0][0][:, 0, :], [xri[0][1]])
                dbg(f"gb{g}", gb[:, 0, :], [('gb', 0, 0), ('gb', 0, 1), ('ARX',)])
            if hide_now:
                while mq[0] < 48 or mpend:
                    mods_step()
                mods_finish(l + 1, 7)
            if not lat:
                dma('sp', fin_d[l].rearrange("t p n -> p t n"), FIN[:].rearrange("p t s d i -> p t (s d i)"), [('FIN',)], [('fin', l)], "fino")
            R.barrier()
            AR.reset(m3)
            sig = [(AR.alloc([128, 512], F32), ('sig', i)) for i in range(2)]
            gtmp = [(AR.alloc([128, 512], F32), ('gtmp', i)) for i in range(2)]
            pgl = Pool([2, 3, 4, 5, 6, 7])
            wgv = kview(w_glu[l])
            for c4 in range(2):
                wv, wk = wload(wgv[:, :, c4 * 512:(c4 + 1) * 512], 8, 512)
                for j in range(4):
                    co = c4 * 4 + j
                    bks = pgl.get(2)
                    for kc in range(8):
                        for n_ in range(2):
                            mm(PS(bks[n_]), wv[:, kc, j * 128:(j + 1) * 128], gb[:, kc, n_ * 512:(n_ + 1) * 512], kc == 0, kc == 7,
                               [wk, ('gb', kc, n_), ('ARX',)], [PK(bks[n_])])
                    for n_ in range(2):
                        sl = slice(n_ * 512, (n_ + 1) * 512)
                        sg, sgk = sig[n_]
                        act(sg[:], PS(bks[n_]), AF.Sigmoid, [PK(bks[n_]), ('vecs',)], [sgk], bias=vecs[:, l, V_BGLU + co:V_BGLU + co + 1])
                        tt(ub[:, co, sl], gb[:, co, sl], sg[:], ALU.mult, [('gb', co, n_), sgk, ('ARX',)], [('ub', co, n_), ('ARX',)])
            if l == 0:
                dbg(f"so{g}", ub[:, 0:2, :], [('ARX',)])
            branch_out(w_so[l], [(ub[:, ct, :], [('ub', ct, 0), ('ub', ct, 1)]) for ct in range(8)], C_GS, False, sig, gtmp)

            if l == 0:
                dbg(f"mixs{g}", MIX[:, 0:2, :], [('MIX', 0), ('MIX', 1)])
            R.barrier()
            AR.reset(a_mark)
            mixb = AR.alloc([128, 16, NT], BF16)
            for c in range(16):
                cp(mixb[:, c, :], MIX[:, c, :], [('MIX', c)], [('mixb', c)], eng='act' if c % 2 else 'dve')
            wov = kview(w_out[l])
            po = Pool([2, 3, 4, 5, 6, 7])
            for oc2 in range(8):
                wv, wk = wload(wov[:, :, oc2 * 256:(oc2 + 1) * 256], 16, 256)
                for j in range(2):
                    oc = oc2 * 2 + j
                    bks = po.get(2)
                    for kc in range(16):
                        for n_ in range(2):
                            mm(PS(bks[n_]), wv[:, kc, j * 128:(j + 1) * 128], mixb[:, kc, n_ * 512:(n_ + 1) * 512], kc == 0, kc == 15,
                               [wk, ('mixb', kc)], [PK(bks[n_])])
                    extra = [('ARX',)] if oc >= 8 else [('MIX', 2 * oc), ('MIX', 2 * oc + 1)]
                    for n_ in range(2):
                        cp(XB[:, oc, n_ * 512:(n_ + 1) * 512], PS(bks[n_]), [PK(bks[n_])], [('XB', oc)] + extra, eng='act' if n_ else 'dve')
            rms_rstd([(XB[:, c, :], [('XB', c)]) for c in range(16)], D, NT, [0, 1], sqt, rstd, rtmp, rtk)
            for c in range(16):
                t_, tk = xt_[0]
                dma('sp', t_[:], xs_d[c * 128:(c + 1) * 128, :], [('xs',)], [tk], "xld")
                tt(XB[:, c, :], XB[:, c, :], rstd[:], ALU.mult, [('XB', c)] + RK, [('XB', c)])
                stt(XB[:, c, :], XB[:, c, :], mA[:, g, 2, c:c + 1], t_[:], ALU.mult, ALU.add, [('XB', c), tk, ('mA', g, 2)], [('XB', c)])

            if l == 0:
                dbg(f"x1{g}", XB[:, 0:2, :], [('XB', 0), ('XB', 1)])
            AR.reset(a_mark)
            hid = AR.alloc([128, 16, 512], BF16)
            fbuf = AR.alloc([128, 16, 512], F32)
            rl = [(AR.alloc([128, 512], F32), ('rl', i)) for i in range(2)]
            rms_rstd([(XB[:, c, :], [('XB', c)]) for c in range(16)], D, NT, [0, 1], sqt, rstd, rtmp, rtk)
            for c in range(16):
                t_, tk = xt_[0]
                tt(t_[:], XB[:, c, :], rstd[:], ALU.mult, [('XB', c)] + RK, [tk])
                act(H[:, c, :], t_[:], AF.Identity, [tk, ('mA', g, 3), ('mA', g, 4)], [('H', c)],
                    bias=mA[:, g, 4, c:c + 1], scale=mA[:, g, 3, c:c + 1])
            ring3[0] = True
            f1v = kview(w_ff1[l])
            f2v = kview(w_ff2[l])
            pf = Pool([2, 3, 4, 5, 6, 7])
            for tt_i in range(2):
                tsl = slice(tt_i * 512, (tt_i + 1) * 512)
                for hq in range(4):
                    for hp in range(8):
                        c0_ = (hq * 16 + hp * 2) * 128
                        wv, wk = wload(f1v[:, :, c0_:c0_ + 256], 16, 256)
                        for j in range(2):
                            hc = hp * 2 + j
                            bk = pf.get(1)[0]
                            for kc in range(16):
                                mm(PS(bk), wv[:, kc, j * 128:(j + 1) * 128], H[:, kc, tsl], kc == 0, kc == 15, [wk, ('H', kc)], [PK(bk)])
                            r_, rk_ = rl[hc % 2]
                            act(r_[:], PS(bk), AF.Relu, [PK(bk)], [rk_])
                            tt(hid[:, hc, :], r_[:], r_[:], ALU.mult, [rk_], [('hid', hc)])
                    for oc2 in range(8):
                        wv, wk = wload(f2v[:, hq * 16:(hq + 1) * 16, oc2 * 256:(oc2 + 1) * 256], 16, 256)
                        for j in range(2):
                            oc = oc2 * 2 + j
                            bk = pf.get(1)[0]
                            for hc in range(16):
                                mm(PS(bk), wv[:, hc, j * 128:(j + 1) * 128], hid[:, hc, :], hc == 0, hc == 15, [wk, ('hid', hc)], [PK(bk)])
                            if hq == 0:
                                cp(fbuf[:, oc, :], PS(bk), [PK(bk)], [('f', oc)], eng='act')
                            else:
                                tt(fbuf[:, oc, :], fbuf[:, oc, :], PS(bk), ALU.add, [('f', oc), PK(bk)], [('f', oc)])
                rms_rstd([(fbuf[:, c, :], [('f', c)]) for c in range(16)], D, 512, [0], sqt, rstd, rtmp, rtk)
                for c in range(16):
                    tt(fbuf[:, c, :], fbuf[:, c, :], rstd[:, 0:512], ALU.mult, [('f', c), ('rstd', 0)], [('f', c)])
                    stt(XB[:, c, tsl], fbuf[:, c, :], mA[:, g, 5, c:c + 1], XB[:, c, tsl], ALU.mult, ALU.add,
                        [('f', c), ('XB', c), ('mA', g, 5)], [('XB', c)])
            ring3[0] = False
            wslot[0] = 0
        dma('sp', yT[g].rearrange("(c p) n -> p c n", p=128), XB[:], XBK, [('yT', g)], "yout")

    nsem = R.emit(nc)
    return nc, len(R.ops), nsem


def _fm(v, nch):
    s = v.shape[:-1]
    return np.moveaxis(v.reshape(s + (nch, 128)), -1, 0)


_CACHE = {}


def make_in_maps(x_prompt, x_sample, cache_ckv, cache_kpe, state_ssm_re, state_ssm_im, c, c_ctx,
           w_mod, b_mod, g_mix_pre, g_mix_post, g_mlp_pre, g_mlp_post, w_in, g_q, w_uq,
           g_kv, w_ukv, w_attn_o, conv_w, conv_b, conv_ln_g, conv_ln_b, w_conv_o,
           ssm_a_re, ssm_a_im, ssm_log_dt, ssm_b_re, ssm_b_im, ssm_c_re, ssm_c_im, ssm_d,
           w_glu, b_glu, w_ssm_o, w_out, w_ff1, w_ff2):
    f32 = np.float32
    A = lambda a: np.ascontiguousarray(np.asarray(a, dtype=f32))
    L = L4

    w_in = A(w_in)
    perm = np.concatenate([np.arange(16, 32), np.arange(0, 16), np.arange(48, 64), np.arange(32, 48)])
    w_ksw = A(w_in[:, :, C_KPE:C_KPE + 64][:, :, perm])
    w_uq = A(w_uq).reshape(L, 512, NH, 192)
    w_ukv = A(w_ukv).reshape(L, 512, NH, 256)
    w_head = np.empty((L, NH, 512, 512), f32)
    for h in range(NH):
        w_head[:, h, :, 0:128] = w_uq[:, :, h, 0:128]
        w_head[:, h, :, 128:192] = w_uq[:, :, h, 128:192]
        w_head[:, h, :, 192:256] = w_uq[:, :, h, 128:192][:, :, perm]
        w_head[:, h, :, 256:384] = w_ukv[:, :, h, 0:128]
        w_head[:, h, :, 384:512] = w_ukv[:, :, h, 128:256]
    vecs = np.zeros((128, L, NV), f32)
    for col, v, nch in ((V_GPRE, g_mix_pre, 16), (V_GPOST, g_mix_post, 16), (V_GMPRE, g_mlp_pre, 16),
                        (V_GMPOST, g_mlp_post, 16), (V_BMOD, b_mod, 96), (V_GQ, g_q, 4), (V_GKV, g_kv, 4),
                        (V_CB, conv_b, 8), (V_LNG, conv_ln_g, 8), (V_LNB, conv_ln_b, 8), (V_SD, ssm_d, 8),
                        (V_BGLU, b_glu, 8)):
        vecs[:, :, col:col + nch] = _fm(A(v), nch)
    cw = A(conv_w)
    vecs[:, :, V_CW:V_CW + 248] = np.transpose(cw.reshape(L, 31, 8, 128), (3, 0, 2, 1)).reshape(128, L, 248)
    def sm(a):
        return np.transpose(a.reshape(L, 2, 32, 2, 64), (0, 3, 4, 1, 2)).reshape(L, 128, 64)
    ldt = np.broadcast_to(A(ssm_log_dt)[:, :, :, None], (L, 2, 64, 64))
    ssm_s = A(np.stack([sm(A(ssm_a_re)), sm(A(ssm_a_im)), sm(A(ldt))], axis=1))
    bre = A(ssm_b_re).reshape(L, 2, 8, 8, 64, 16)
    bim = A(ssm_b_im).reshape(L, 2, 8, 8, 64, 16)
    b_x = np.zeros((L, 2, 8, 2, 128, 512), f32)
    for gl in range(8):
        b_x[:, :, :, 0, gl * 16:(gl + 1) * 16, gl * 64:(gl + 1) * 64] = np.swapaxes(bre[:, :, :, gl], -1, -2)
        b_x[:, :, :, 1, gl * 16:(gl + 1) * 16, gl * 64:(gl + 1) * 64] = np.swapaxes(bim[:, :, :, gl], -1, -2)
    cre = A(ssm_c_re).reshape(L, 2, 32, 2, 16, 64)
    cim = A(ssm_c_im).reshape(L, 2, 32, 2, 16, 64)
    c_x = np.zeros((L, 2, 32, 2, 128, 128), f32)
    for t in range(32):
        for g2 in range(2):
            gl = (2 * t + g2) % 8
            c_x[:, :, t, 0, g2 * 64:(g2 + 1) * 64, gl * 16:(gl + 1) * 16] = np.swapaxes(cre[:, :, t, g2], -1, -2)
            c_x[:, :, t, 1, g2 * 64:(g2 + 1) * 64, gl * 16:(gl + 1) * 16] = np.swapaxes(cim[:, :, t, g2], -1, -2)
    pos = np.arange(NT)
    inv = (10000.0 ** (-np.arange(16, dtype=np.float32) / 16)).astype(f32)
    ang = np.stack([(pos // 64).astype(f32)[:, None] * inv, (pos % 64).astype(f32)[:, None] * inv], axis=1)
    cos, sin = np.cos(ang).astype(f32), np.sin(ang).astype(f32)
    rope = np.zeros((2, 64, NT), f32)
    for ax in range(2):
        for hf in range(2):
            rows = slice(ax * 32 + hf * 16, ax * 32 + hf * 16 + 16)
            rope[0, rows, :] = cos[:, ax, :].T
            rope[1, rows, :] = (-sin[:, ax, :].T) if hf == 0 else sin[:, ax, :].T
    shared = dict(w_mod=A(w_mod), w_in=w_in, w_ksw=w_ksw, w_head=w_head, w_ao=A(w_attn_o), w_co=A(w_conv_o),
                  w_glu=A(w_glu), w_so=A(w_ssm_o), w_out=A(w_out), w_ff1=A(w_ff1), w_ff2=A(w_ff2), vecs=vecs,
                  ssm_s=ssm_s, b_x=b_x, c_x=c_x, rope=rope, ident=np.eye(128, dtype=f32))
    xp, xsm = A(x_prompt), A(x_sample)
    cc, cctx = A(c), A(c_ctx)
    ckv, kpe = A(cache_ckv), A(cache_kpe)
    sre, sim_ = A(state_ssm_re), A(state_ssm_im)
    in_maps = []
    for core in range(8):
        xTc = np.empty((2, D, NT), f32)
        xTc[0] = xp[4 * core:4 * core + 4].reshape(NT, D).T
        xTc[1] = xsm[core].T
        cTc = np.stack([_fm(cctx, 16), _fm(cc[core], 16)], axis=-1)
        m = dict(shared)
        m.update(xT=xTc, cT=A(cTc), ckvc=A(np.swapaxes(ckv[core], -1, -2)), kpec=A(np.swapaxes(kpe[core], -1, -2)),
                 h0=A(np.stack([sm(sre[core]), sm(sim_[core])], axis=1)))
        in_maps.append(m)
    return in_maps


def assemble(results):
    f32 = np.float32
    L = L4
    y_prompt = np.empty((32, 256, D), f32)
    y_sample = np.empty((8, 1024, D), f32)
    new_ckv = np.empty((32, L, 256, 512), f32)
    new_kpe = np.empty((32, L, 256, 64), f32)
    new_re = np.empty((32, L, 2, 64, 64), f32)
    new_im = np.empty((32, L, 2, 64, 64), f32)
    for core in range(8):
        r = results[core]
        y_prompt[4 * core:4 * core + 4] = r["yT"][0].T.reshape(4, 256, D)
        y_sample[core] = r["yT"][1].T
        new_ckv[4 * core:4 * core + 4] = np.transpose(r["ckvT"].reshape(L, 512, 4, 256), (2, 0, 3, 1))
        new_kpe[4 * core:4 * core + 4] = np.transpose(r["kpeT"].reshape(L, 64, 4, 256), (2, 0, 3, 1))
        fin = r["fin"].reshape(L, 2, 2, 64, 4, 2, 32)
        fin = np.transpose(fin, (1, 4, 0, 5, 6, 2, 3)).reshape(2, 4, L, 2, 64, 64)
        new_re[4 * core:4 * core + 4] = fin[0]
        new_im[4 * core:4 * core + 4] = fin[1]
    return (y_prompt, y_sample, new_ckv, new_kpe, new_re, new_im)


def kernel(**inputs):
    if 'nc' not in _CACHE:
        _CACHE['nc'] = build_program()
    nc, nops, nsem = _CACHE['nc']
    in_maps = make_in_maps(**inputs)
    res = run_bass_kernel_spmd(nc, in_maps, core_ids=list(range(8)))
    return assemble(res.results)
```
